# Optimizing a Trainium2 kernel written in Bass

```python
import math
import jax, jax.numpy as jnp
from jax import lax
import numpy as np

D_MODEL = 1024
BATCH = 16
SEQ = 2048
DEPTH = 2

CHUNK = 64
N_MIXERS = 2
N_S5_LAYERS = (DEPTH + 1) // 2
N_SGU_LAYERS = DEPTH // 2
BRANCH = D_MODEL
S5_GROUP = 16
S5_GROUPS = BRANCH // S5_GROUP
S5_STATE = 64
SGU_BLOCK = 128
SGU_HEADS = 8
SGU_HEAD_DIM = BRANCH // SGU_HEADS
RMS_EPS = 1e-6
LN_EPS = 1e-5
DT_MIN = 1e-3
DT_MAX = 1e-1
RE_CLIP = -1e-4

kernel_name = "hybrid_s5_gmlp_streaming_trunk"


def rmsnorm(x, g):
    xf = x.astype(jnp.float32)
    y = xf * lax.rsqrt(jnp.mean(xf * xf, axis=-1, keepdims=True) + RMS_EPS)
    return (y * g.astype(jnp.float32)).astype(x.dtype)


def layernorm(x, g, b):
    xf = x.astype(jnp.float32)
    mu = jnp.mean(xf, axis=-1, keepdims=True)
    var = jnp.mean(jnp.square(xf - mu), axis=-1, keepdims=True)
    y = (xf - mu) * lax.rsqrt(var + LN_EPS)
    return (y * g.astype(jnp.float32) + b.astype(jnp.float32)).astype(x.dtype)


def _linear_recurrence(left, right):
    a_l, b_l = left
    a_r, b_r = right
    return a_r * a_l, a_r * b_l + b_r


def s5_mixer(u, A_re, A_im, log_dt, B_re, B_im, C_re, C_im, D):
    bsz, seq, _ = u.shape
    uf = u.astype(jnp.float32)
    lam = lax.complex(jnp.minimum(A_re.astype(jnp.float32), RE_CLIP), A_im.astype(jnp.float32))
    dt = jnp.exp(log_dt.astype(jnp.float32))[:, None]
    lam_bar = jnp.exp(lam * dt)
    b_cplx = lax.complex(B_re.astype(jnp.float32), B_im.astype(jnp.float32))
    b_bar = ((lam_bar - 1.0) / lam)[..., None] * b_cplx
    c_cplx = lax.complex(C_re.astype(jnp.float32), C_im.astype(jnp.float32))
    steps = jnp.arange(1, CHUNK + 1, dtype=jnp.float32)[:, None, None]
    lam_pow = jnp.exp(lam[None] * dt[None] * steps)
    a_elems = jnp.broadcast_to(lam_bar[None, None], (CHUNK, bsz, S5_GROUPS, S5_STATE))
    n_chunks = seq // CHUNK
    u_chunks = uf.reshape(bsz, n_chunks, CHUNK, S5_GROUPS, S5_GROUP).transpose(1, 2, 0, 3, 4)

    def step(h_prev, u_c):
        bu = jnp.einsum('lbgh,gph->lbgp', u_c, b_bar)
        _, h = lax.associative_scan(_linear_recurrence, (a_elems, bu), axis=0)
        h = h + lam_pow[:, None] * h_prev[None]
        y = jnp.real(jnp.einsum('lbgp,ghp->lbgh', h, c_cplx))
        return h[-1], y

    h0 = jnp.zeros((bsz, S5_GROUPS, S5_STATE), dtype=jnp.complex64)
    _, ys = lax.scan(step, h0, u_chunks)
    y = ys.transpose(2, 0, 1, 3, 4).reshape(bsz, seq, BRANCH)
    return y + D.astype(jnp.float32) * uf


def sgu_mixer(u, v, ln_g, ln_b, w_s, b_s):
    bsz, seq, _ = v.shape
    vn = layernorm(v, ln_g, ln_b)
    vb = vn.reshape(bsz, seq // SGU_BLOCK, SGU_BLOCK, SGU_HEADS, SGU_HEAD_DIM)
    mask = jnp.tril(jnp.ones((SGU_BLOCK, SGU_BLOCK), dtype=bool))
    w = jnp.where(mask[None], w_s, jnp.zeros_like(w_s))
    mixed = jnp.einsum('hts,bnshc->bnthc', w, vb) + jnp.transpose(b_s)[:, :, None]
    return u * mixed.reshape(bsz, seq, BRANCH)


def setup_inputs(seed: int = 0) -> dict:
    key = jax.random.key(seed)
    ks = jax.random.split(key, 24)
    f32 = jnp.float32
    nA, nB, G, P, H = N_S5_LAYERS, N_SGU_LAYERS, S5_GROUPS, S5_STATE, S5_GROUP
    x = jax.random.normal(ks[0], (BATCH, SEQ, D_MODEL), f32)
    norm_g = 1.0 + 0.02 * jax.random.normal(ks[1], (DEPTH, D_MODEL), f32)
    final_g = 1.0 + 0.02 * jax.random.normal(ks[2], (D_MODEL,), f32)
    s5_w_in = jax.random.normal(ks[3], (nA, D_MODEL, 2 * BRANCH), f32) * D_MODEL ** -0.5
    n_idx = jnp.arange(P, dtype=f32)
    s5_A_re = -0.5 + 0.01 * jax.random.normal(ks[4], (nA, G, P), f32)
    s5_A_im = math.pi * n_idx + 0.01 * jax.random.normal(ks[5], (nA, G, P), f32)
    s5_log_dt = jax.random.uniform(ks[6], (nA, G), f32, math.log(DT_MIN), math.log(DT_MAX))
    s5_B_re = jax.random.normal(ks[7], (nA, G, P, H), f32) * (2.0 * H) ** -0.5
    s5_B_im = jax.random.normal(ks[8], (nA, G, P, H), f32) * (2.0 * H) ** -0.5
    s5_C_re = jax.random.normal(ks[9], (nA, G, H, P), f32) * (2.0 * P) ** -0.5
    s5_C_im = jax.random.normal(ks[10], (nA, G, H, P), f32) * (2.0 * P) ** -0.5
    s5_D = jax.random.normal(ks[11], (nA, BRANCH), f32)
    s5_w_glu = jax.random.normal(ks[12], (nA, BRANCH, BRANCH), f32) * BRANCH ** -0.5
    s5_b_glu = 0.01 * jax.random.normal(ks[13], (nA, BRANCH), f32)
    s5_w_out = jax.random.normal(ks[14], (nA, BRANCH, D_MODEL), f32) * BRANCH ** -0.5
    sgu_w_in = jax.random.normal(ks[15], (nB, D_MODEL, 3 * BRANCH), f32) * D_MODEL ** -0.5
    sgu_ln_g = 1.0 + 0.02 * jax.random.normal(ks[16], (nB, BRANCH), f32)
    sgu_ln_b = 0.01 * jax.random.normal(ks[17], (nB, BRANCH), f32)
    sgu_w_s = jax.random.normal(ks[18], (nB, SGU_HEADS, SGU_BLOCK, SGU_BLOCK), f32) * SGU_BLOCK ** -0.5
    sgu_b_s = 1.0 + 0.01 * jax.random.normal(ks[19], (nB, SGU_HEADS, SGU_BLOCK), f32)
    sgu_w_out = jax.random.normal(ks[20], (nB, BRANCH, D_MODEL), f32) * BRANCH ** -0.5
    return {"x": x, "norm_g": norm_g, "final_g": final_g,
            "s5_w_in": s5_w_in, "s5_A_re": s5_A_re, "s5_A_im": s5_A_im, "s5_log_dt": s5_log_dt,
            "s5_B_re": s5_B_re, "s5_B_im": s5_B_im, "s5_C_re": s5_C_re, "s5_C_im": s5_C_im,
            "s5_D": s5_D, "s5_w_glu": s5_w_glu, "s5_b_glu": s5_b_glu, "s5_w_out": s5_w_out,
            "sgu_w_in": sgu_w_in, "sgu_ln_g": sgu_ln_g, "sgu_ln_b": sgu_ln_b,
            "sgu_w_s": sgu_w_s, "sgu_b_s": sgu_b_s, "sgu_w_out": sgu_w_out}


def reference(x, norm_g, final_g,
              s5_w_in, s5_A_re, s5_A_im, s5_log_dt, s5_B_re, s5_B_im, s5_C_re, s5_C_im,
              s5_D, s5_w_glu, s5_b_glu, s5_w_out,
              sgu_w_in, sgu_ln_g, sgu_ln_b, sgu_w_s, sgu_b_s, sgu_w_out):
    for i in range(DEPTH):
        h = rmsnorm(x, norm_g[i])
        j = i // N_MIXERS
        if i % N_MIXERS == 0:
            proj = h @ s5_w_in[j]
            branch, gate = jnp.split(proj, 2, axis=-1)
            y = s5_mixer(branch, s5_A_re[j], s5_A_im[j], s5_log_dt[j], s5_B_re[j], s5_B_im[j],
                         s5_C_re[j], s5_C_im[j], s5_D[j]).astype(x.dtype)
            y = jax.nn.gelu(y)
            y = y * jax.nn.sigmoid(y @ s5_w_glu[j] + s5_b_glu[j])
            out = (y * jax.nn.silu(gate)) @ s5_w_out[j]
        else:
            proj = h @ sgu_w_in[j]
            u, v, gate = jnp.split(proj, 3, axis=-1)
            y = sgu_mixer(jax.nn.gelu(u), jax.nn.gelu(v), sgu_ln_g[j], sgu_ln_b[j],
                          sgu_w_s[j], sgu_b_s[j])
            out = (y * jax.nn.silu(gate)) @ sgu_w_out[j]
        x = x + out
    return rmsnorm(x, final_g)
```

```python
import math
from contextlib import ExitStack
import numpy as np
import concourse.bass as bass
import concourse.mybir as mybir
from concourse.bass_utils import run_bass_kernel_spmd

F32 = mybir.dt.float32
BF16 = mybir.dt.bfloat16
AF = mybir.ActivationFunctionType
ALU = mybir.AluOpType

NCORES = 8
SEQ = 2048
D = 1024
NSTEP = 4
TS = 512
PI = math.pi


class Prog:
    ENG = ["pe", "act", "dve", "pool", "sp"]

    def __init__(self):
        self.ops = {e: [] for e in self.ENG}
        self.lastw = {}
        self.rd = {}
        self.dmacnt = {}

    def add(self, eng, fn, reads=(), writes=(), dma=None, waitfor=(), noself=False):
        deps = []
        for r in waitfor:
            w = self.lastw.get(r)
            if w is not None:
                deps.append(w)
            for e in self.rd.get(r, ()):
                if e[0] == "E" and e[1] == eng:
                    continue
                deps.append(e)
        for r in reads:
            w = self.lastw.get(r)
            if w is not None:
                deps.append(w)
        for r in writes:
            w = self.lastw.get(r)
            if w is not None:
                deps.append(w)
            for e in self.rd.get(r, ()):
                if e[0] == "E" and e[1] == eng:
                    continue
                deps.append(e)
        idx = len(self.ops[eng])
        if dma is not None:
            c = self.dmacnt.get(dma, 0) + 1
            self.dmacnt[dma] = c
            ev = ("D", dma, c)
        else:
            ev = ("E", eng, idx)
        keep = []
        seen = set()
        for d in deps:
            if d in seen:
                continue
            seen.add(d)
            if d[0] == "E" and d[1] == eng and (eng in ("pe", "sp") or noself):
                continue
            keep.append(d)
        self.ops[eng].append({"fn": fn, "deps": keep, "ev": ev, "signal": False})
        for r in reads:
            lst = self.rd.setdefault(r, [])
            if ev[0] == "E":
                lst[:] = [x for x in lst if not (x[0] == "E" and x[1] == eng)]
            else:
                lst[:] = [x for x in lst if not (x[0] == "D" and x[1] == ev[1])]
            lst.append(ev)
        for r in writes:
            self.lastw[r] = ev
            self.rd[r] = []
        return ev

    def barrier(self):
        lasts = []
        for e in self.ENG:
            for i in range(len(self.ops[e]) - 1, -1, -1):
                o = self.ops[e][i]
                if o["fn"] is not None and o["ev"][0] == "E":
                    lasts.append(o["ev"])
                    break
        dm = [("D", k, c) for k, c in self.dmacnt.items()]
        for e in self.ENG:
            deps = [d for d in lasts if d[1] != e] + dm
            self.ops[e].append({"fn": None, "deps": deps, "ev": ("E", e, len(self.ops[e])), "signal": False})
        self.lastw = {}
        self.rd = {}

    def finalize(self):
        for e in self.ENG:
            for o in self.ops[e]:
                for d in o["deps"]:
                    if d[0] == "E":
                        self.ops[d[1]][d[2]]["signal"] = True
        self.sigval = {}
        for e in self.ENG:
            c = 0
            for i, o in enumerate(self.ops[e]):
                if o["signal"]:
                    c += 1
                    self.sigval[(e, i)] = c

    def emit_engine(self, e, eng, esem, dsem):
        waited = {}
        for o in self.ops[e]:
            need = {}
            for d in o["deps"]:
                if d[0] == "E":
                    key = ("E", d[1])
                    val = self.sigval[(d[1], d[2])]
                else:
                    key = ("D", d[1])
                    val = 16 * d[2]
                if val > need.get(key, 0):
                    need[key] = val
            for key, val in need.items():
                if waited.get(key, 0) < val:
                    sem = esem[key[1]] if key[0] == "E" else dsem[key[1]]
                    eng.wait_ge(sem, val)
                    waited[key] = val
            if o["fn"] is None:
                continue
            ins = o["fn"](eng)
            if o["ev"][0] == "D":
                ins.then_inc(dsem[o["ev"][1]], 16)
            elif o["signal"]:
                ins.then_inc(esem[e], 1)


def build_nc(debug=False):
    nc = bass.Bass("TRN2", target_bir_lowering=False)
    P = Prog()

    def din(name, shape):
        return nc.dram_tensor(name, list(shape), F32, kind="ExternalInput").ap()

    x2 = din("x2", [2, SEQ, D])
    gB0 = din("gB0", [128, D])
    gB1 = din("gB1", [128, D])
    gBf = din("gBf", [128, D])
    w_in0 = din("w_in0", [D, 2 * D])
    w_glu = din("w_glu", [D, D])
    w_out0 = din("w_out0", [D, D])
    w_in1 = din("w_in1", [D, 3 * D])
    w_out1 = din("w_out1", [D, D])
    dAre = din("Are", [128, 32])
    dAim = din("Aim", [128, 32])
    dLdt = din("Ldt", [128, 32])
    dBre = din("Bre", [128, 512])
    dBim = din("Bim", [128, 512])
    dCre = din("Cre", [128, 512])
    dCim = din("Cim", [128, 512])
    dDl = din("Dl", [128, 8])
    dbglu = din("bglu", [128, 8])
    dWs = din("Ws", [128, 1024])
    dtril = din("tril", [128, 128])
    dlng = din("lng", [128, 8])
    dlnb = din("lnb", [128, 8])
    dbsB = din("bsB", [128, 1024])
    dident = din("ident", [128, 128])
    y2 = nc.dram_tensor("y2", [2, SEQ, D], F32, kind="ExternalOutput").ap()
    wscr = nc.dram_tensor("wscr", [16, 128, 8, 512], BF16, kind="Internal").ap()

    es = ExitStack()

    def sb(name, shape, dt):
        return es.enter_context(nc.sbuf_tensor("sb_" + name, list(shape), dt))

    XV = sb("XV", [128, 8192], F32)
    hT = sb("hT", [128, 8, 1024], BF16)
    A = sb("A", [128, 8, 1024], BF16)
    Bz = sb("Bz", [128, 8, 1024], BF16)
    TK = sb("TK", [128, 8, 8, 128], BF16)
    WIN = sb("WIN", [128, 8, 16, 128], BF16)
    WOUT = sb("WOUT", [128, 32, 8, 2, 32], BF16)
    ARI = sb("ARI", [128, 2, 2, 32], F32)
    ring = sb("ring", [128, 2, 8, 512], BF16)
    identb = sb("identb", [128, 128], BF16)
    onesb = sb("onesb", [128, 128], BF16)
    gB0b = sb("gB0b", [128, D], BF16)
    gB1b = sb("gB1b", [128, D], BF16)
    gBf32 = sb("gBf32", [128, D], F32)
    Dl = sb("Dl", [128, 8], F32)
    bglu = sb("bglu", [128, 8], F32)
    lng = sb("lng", [128, 8], F32)
    lnb = sb("lnb", [128, 8], F32)
    BiasT = sb("BiasT", [128, 8, 128], F32)
    WmT = sb("WmT", [128, 8, 128], BF16)
    xs = sb("xs", [128, 2, 1024], BF16)
    stat = sb("stat", [128, 112], F32)
    S12 = sb("S12", [128, 8, 2, 2], F32)
    hst = sb("hst", [128, 3, 64], F32)
    rt = sb("rt", [128, 3, 2, 64], F32)
    ev32 = sb("ev32", [128, 2, 512], F32)
    evb = sb("evb", [128, 2, 512], BF16)
    epsc = sb("epsc", [128, 2], F32)
    misc32 = sb("misc32", [128, 704], F32)
    PS = es.enter_context(nc.psum_tensor("PS", [128, 6, 512], F32))
    PT = es.enter_context(nc.psum_tensor("PT", [128, 2, 1024], BF16))

    def dma(q, out, in_, reads, writes, key, waitfor=()):
        P.add(q, lambda e, out=out, in_=in_: e.dma_start(out=out, in_=in_), reads, writes, dma=key, waitfor=waitfor)

    def mm(out, lhsT, rhs, start, stop, reads, writes, tp=None, sgc=False):
        def fn(e, out=out, lhsT=lhsT, rhs=rhs, start=start, stop=stop, tp=tp, sgc=sgc):
            kw = {}
            if tp is not None:
                kw["tile_position"] = tp
            if sgc:
                kw["skip_group_check"] = True
            return e.matmul(out, lhsT=lhsT, rhs=rhs, start=start, stop=stop, **kw)
        P.add("pe", fn, reads, writes)

    def tr(out, in_, reads, writes):
        P.add("pe", lambda e, out=out, in_=in_: e.transpose(out, in_, identb[:]), list(reads) + ["ident"], writes)

    def act(out, in_, func, reads, writes, bias=None, scale=None, accum=None, waitfor=()):
        def fn(e, out=out, in_=in_, func=func, bias=bias, scale=scale, accum=accum):
            kw = {}
            if bias is not None:
                kw["bias"] = bias
            if scale is not None:
                kw["scale"] = scale
            if accum is not None:
                kw["accum_out"] = accum
            return e.activation(out=out, in_=in_, func=func, **kw)
        P.add("act", fn, reads, writes, waitfor=waitfor)

    def cp(eng, out, in_, reads, writes, waitfor=()):
        if eng == "act":
            act(out, in_, AF.Copy, reads, writes, waitfor=waitfor)
        else:
            P.add(eng, lambda e, out=out, in_=in_: e.tensor_copy(out=out, in_=in_), reads, writes, waitfor=waitfor)

    def tt(eng, out, in0, in1, op, reads, writes, waitfor=()):
        P.add(eng, lambda e, out=out, in0=in0, in1=in1, op=op: e.tensor_tensor(out=out, in0=in0, in1=in1, op=op), reads, writes,
              waitfor=waitfor)

    def ts(eng, out, in0, s1, s2, op0, op1, reads, writes):
        def fn(e, out=out, in0=in0, s1=s1, s2=s2, op0=op0, op1=op1):
            if op1 is None:
                return e.tensor_scalar(out=out, in0=in0, scalar1=s1, scalar2=None, op0=op0)
            return e.tensor_scalar(out=out, in0=in0, scalar1=s1, scalar2=s2, op0=op0, op1=op1)
        P.add(eng, fn, reads, writes)

    def stt(eng, out, in0, scalar, in1, op0, op1, reads, writes, accum=None):
        def fn(e, out=out, in0=in0, scalar=scalar, in1=in1, op0=op0, op1=op1, accum=accum):
            if accum is None:
                return e.scalar_tensor_tensor(out=out, in0=in0, scalar=scalar, in1=in1, op0=op0, op1=op1)
            return e.scalar_tensor_tensor(out=out, in0=in0, scalar=scalar, in1=in1, op0=op0, op1=op1, accum_out=accum)
        P.add(eng, fn, reads, writes)

    def memset(eng, ap, val, writes):
        P.add(eng, lambda e, ap=ap, val=val: e.memset(ap, val), (), writes)

    def recip(out, in_, reads, writes):
        P.add("dve", lambda e, out=out, in_=in_: e.reciprocal(out=out, in_=in_), reads, writes)

    XVR = ["XV%d" % r for r in range(8)]
    HBR = ["hT%d_%d" % (m_, kh_) for m_ in range(8) for kh_ in range(2)]
    hTf = hT[:].rearrange("p k f -> p (k f)")

    def xsub(m):
        return XV[:, m * 1024:(m + 1) * 1024]

    def Vv(ri):
        return XV[:, ri * 4096:(ri + 1) * 4096].rearrange("p (q b c) -> p q b c", q=32, b=2)

    def Hv(ri):
        return hTf[:, ri * 4096:(ri + 1) * 4096].rearrange("p (q b c) -> p q b c", q=32, b=2)

    t32 = XV
    off = [0]

    def t32a(n):
        a = t32[:, off[0]:off[0] + n]
        off[0] += n
        return a

    offb = [0]

    def t32b(n):
        a = misc32[:, offb[0]:offb[0] + n]
        offb[0] += n
        return a

    idt32 = t32b(128)
    dma("sp", idt32, dident, (), ["pp_id"], "pp0")
    Are = t32b(32); Aim = t32b(32); Ldt = t32b(32)
    Bre = t32a(512); Bim = t32a(512); Cre = t32a(512); Cim = t32a(512)
    for i_, (t_, d_, nm) in enumerate([(Are, dAre, "Are"), (Aim, dAim, "Aim"), (Ldt, dLdt, "Ldt"), (Bre, dBre, "Bre"),
                                       (Bim, dBim, "Bim"), (Cre, dCre, "Cre"), (Cim, dCim, "Cim")]):
        dma("sp", t_, d_, (), [nm], "pp8_%d" % i_)
    Ws32 = ev32[:].rearrange("p a c -> p (a c)")
    tril32 = t32b(128)
    bsB32 = t32a(1024)
    dma("sp", Ws32, dWs, (), ["Ws32"], "pp5")
    dma("sp", tril32, dtril, (), ["tril"], "pp6")
    dma("sp", bsB32, dbsB, (), ["bsB"], "pp7")
    cp("act", identb[:], idt32, ["pp_id"], ["ident"])
    memset("dve", onesb[:], 1.0, ["ones"])
    memset("dve", epsc[:, 0:1], 1e-6, ["eps"])
    memset("dve", epsc[:, 1:2], 1e-5, ["eps"])
    dma("pool", gB0b[:], gB0, (), ["gB0"], "pp1")
    dma("pool", gB1b[:], gB1, (), ["gB1"], "pp2")
    for i_, (t_, d_, nm) in enumerate([(Dl, dDl, "Dl"), (bglu, dbglu, "bglu"), (lng, dlng, "lng"), (lnb, dlnb, "lnb")]):
        dma("sp", t_[:], d_, (), [nm], "pp4_%d" % i_)
    dtt = t32b(32); lre = t32b(32); xr = t32b(32); xi = t32b(32)
    act(dtt, Ldt, AF.Exp, ["Ldt"], ["dt"])
    ts("dve", lre, Are, -1e-4, None, ALU.min, None, ["Are"], ["lre"])
    tt("dve", xr, lre, dtt, ALU.mult, ["lre", "dt"], ["xr"])
    tt("dve", xi, Aim, dtt, ALU.mult, ["Aim", "dt"], ["xi"])
    TH = t32a(512); SC = t32a(512); MAG = t32a(256); Et = t32a(576)
    THv = TH.rearrange("p (e s q) -> p e s q", e=8, s=2)
    SCv = SC.rearrange("p (e s q) -> p e s q", e=8, s=2)
    MAGv = MAG.rearrange("p (e q) -> p e q", e=8)
    Etv = Et.rearrange("p (e r q) -> p e r q", e=9, r=2)
    for e_ in range(8):
        ts("dve", THv[:, e_, 0, :], xi, float(e_ + 1), None, ALU.mult, None, ["xi"], ["TH"])
        ts("dve", THv[:, e_, 1, :], xi, float(e_ + 1), 0.5 * PI, ALU.mult, ALU.add, ["xi"], ["TH"])
    MAGIC = 12582912.0
    ts("dve", SC, TH, 1.0 / (2 * PI), MAGIC, ALU.mult, ALU.add, ["TH"], ["SC"])
    ts("dve", SC, SC, -MAGIC, None, ALU.add, None, ["SC"], ["SC"])
    stt("dve", TH, SC, -2 * PI, TH, ALU.mult, ALU.add, ["SC", "TH"], ["TH"])
    ts("dve", TH, TH, -PI, PI, ALU.max, ALU.min, ["TH"], ["TH"])
    for e_ in range(8):
        act(MAGv[:, e_, :], xr, AF.Exp, ["xr"], ["MAG"], scale=float(e_ + 1))
    act(SC, TH, AF.Sin, ["TH"], ["SC"])
    memset("dve", Etv[:, 0, 0, :], 1.0, ["Et"])
    memset("dve", Etv[:, 0, 1, :], 0.0, ["Et"])
    tt("dve", Etv[:, 1:9, 0, :], MAGv, SCv[:, :, 1, :], ALU.mult, ["MAG", "SC"], ["Et"])
    tt("dve", Etv[:, 1:9, 1, :], MAGv, SCv[:, :, 0, :], ALU.mult, ["MAG", "SC"], ["Et"])
    cp("dve", ARI[:, 0, 0, :], Etv[:, 8, 0, :], ["Et"], ["ARI"])
    cp("dve", ARI[:, 0, 1, :], Etv[:, 8, 0, :], ["Et"], ["ARI"])
    ts("dve", ARI[:, 1, 0, :], Etv[:, 8, 1, :], -1.0, None, ALU.mult, None, ["Et"], ["ARI"])
    cp("dve", ARI[:, 1, 1, :], Etv[:, 8, 1, :], ["Et"], ["ARI"])
    nr = t32b(32); den = t32b(32); tA = t32b(32); tB = t32b(32); rr = t32b(32); rim = t32b(32)
    ni = Etv[:, 1, 1, :]
    ts("dve", nr, Etv[:, 1, 0, :], -1.0, None, ALU.add, None, ["Et"], ["nr"])
    tt("dve", den, lre, lre, ALU.mult, ["lre"], ["den"])
    tt("dve", tA, Aim, Aim, ALU.mult, ["Aim"], ["tA"])
    tt("dve", den, den, tA, ALU.add, ["den", "tA"], ["den"])
    recip(den, den, ["den"], ["den"])
    tt("dve", tA, nr, lre, ALU.mult, ["nr", "lre"], ["tA"])
    tt("dve", tB, ni, Aim, ALU.mult, ["Et", "Aim"], ["tB"])
    tt("dve", tA, tA, tB, ALU.add, ["tA", "tB"], ["tA"])
    tt("dve", rr, tA, den, ALU.mult, ["tA", "den"], ["rr"])
    tt("dve", tA, ni, lre, ALU.mult, ["Et", "lre"], ["tA"])
    tt("dve", tB, nr, Aim, ALU.mult, ["nr", "Aim"], ["tB"])
    tt("dve", tA, tA, tB, ALU.subtract, ["tA", "tB"], ["tA"])
    tt("dve", rim, tA, den, ALU.mult, ["tA", "den"], ["rim"])
    Bbr = t32a(512); Bbi = t32a(512); u1 = t32a(512); u2 = t32a(512); u3 = t32a(512); u4 = t32a(512)

    def v3(a):
        return a.rearrange("p (q h) -> p q h", q=32)

    def bq(a):
        return a.unsqueeze(2).broadcast_to([128, 32, 16])

    tt("dve", v3(u1), v3(Bre), bq(rr), ALU.mult, ["Bre", "rr"], ["u1"])
    tt("dve", v3(u2), v3(Bim), bq(rim), ALU.mult, ["Bim", "rim"], ["u2"])
    tt("dve", Bbr, u1, u2, ALU.subtract, ["u1", "u2"], ["Bbr"])
    tt("dve", v3(u1), v3(Bim), bq(rr), ALU.mult, ["Bim", "rr"], ["u1"])
    tt("dve", v3(u2), v3(Bre), bq(rim), ALU.mult, ["Bre", "rim"], ["u2"])
    tt("dve", Bbi, u1, u2, ALU.add, ["u1", "u2"], ["Bbi"])
    BBD = hT[:, 0:2, :].rearrange("p a (b c) -> p (a b) c", c=32).rearrange("p (q r) c -> p q r c", r=2)
    ZB0 = hT[:, 2:4, :].rearrange("p a (b c) -> p (a b) c", c=32).rearrange("p (q r) c -> p q r c", r=2)
    memset("pool", hT[:, 0:4, :], 0.0, ["BBD", "ZB0"])
    memset("pool", WOUT[:], 0.0, ["WOUT"])
    P.add("act", lambda e: e.memzero(A[:]), (), ["XBDlo"])
    P.add("act", lambda e: e.memzero(Bz[:]), (), ["XBDhi"])
    memset("pool", TK[:], 0.0, ["TK"])
    for gp in range(2):
        ps_ = slice(64 * gp, 64 * gp + 64)
        cs_ = slice(16 * gp, 16 * gp + 16)
        cp("dve", BBD[ps_, :, 0, cs_], v3(Bbr)[ps_], ["Bbr"], ["BBD"])
        cp("dve", BBD[ps_, :, 1, cs_], v3(Bbi)[ps_], ["Bbi"], ["BBD"])
        cp("dve", ZB0[ps_, :, 0, cs_], v3(Cre)[ps_], ["Cre"], ["ZB0"])
        ts("dve", ZB0[ps_, :, 1, cs_], v3(Cim)[ps_], -1.0, None, ALU.mult, None, ["Cim"], ["ZB0"])
    def XBD(q0, q1):
        buf = A if q0 < 16 else Bz
        v = buf[:].rearrange("p a (b c) -> p (a b) c", c=32).rearrange("p (q s r) c -> p q s r c", s=8, r=2)
        return v, (0 if q0 < 16 else 16)

    for e_ in range(9):
        Er = bq(Etv[:, e_, 0, :]); Ei = bq(Etv[:, e_, 1, :])
        if e_ >= 1:
            tau = e_ - 1
            tt("dve", v3(u1), v3(Cre), Er, ALU.mult, ["Cre", "Et"], ["u1"])
            tt("dve", v3(u2), v3(Cim), Ei, ALU.mult, ["Cim", "Et"], ["u2"])
            tt("dve", v3(u3), v3(Cre), Ei, ALU.mult, ["Cre", "Et"], ["u3"])
            tt("dve", v3(u4), v3(Cim), Er, ALU.mult, ["Cim", "Et"], ["u4"])
            for gp in range(2):
                ps_ = slice(64 * gp, 64 * gp + 64)
                cs_ = slice(16 * gp, 16 * gp + 16)
                tt("dve", WOUT[ps_, :, tau, 0, cs_], v3(u1)[ps_], v3(u2)[ps_], ALU.subtract, ["u1", "u2"], ["WOUT"])
                stt("dve", WOUT[ps_, :, tau, 1, cs_], v3(u3)[ps_], -1.0, v3(u4)[ps_], ALU.mult, ALU.subtract,
                    ["u3", "u4"], ["WOUT"])
        if e_ <= 7:
            sg_ = 7 - e_
            if e_ <= 0:
                en_, w1, w2, w3, w4 = "dve", u1, u2, u3, u4
                n1, n2, n3, n4 = "u1", "u2", "u3", "u4"
            else:
                BTf = BiasT[:].rearrange("p h t -> p (h t)")
                en_, w1, w2, w3, w4 = "pool", gBf32[:, 0:512], gBf32[:, 512:1024], BTf[:, 0:512], BTf[:, 512:1024]
                n1, n2, n3, n4 = "u5", "u6", "BiasT", "BiasT"
            tt(en_, v3(w1), v3(Bbr), Er, ALU.mult, ["Bbr", "Et"], [n1])
            tt(en_, v3(w2), v3(Bbi), Ei, ALU.mult, ["Bbi", "Et"], [n2])
            tt(en_, v3(w3), v3(Bbi), Er, ALU.mult, ["Bbi", "Et"], [n3])
            tt(en_, v3(w4), v3(Bbr), Ei, ALU.mult, ["Bbr", "Et"], [n4])
            for half in range(2):
                xv_, q0 = XBD(16 * half, 16 * half + 16)
                nm = "XBDlo" if half == 0 else "XBDhi"
                qs_ = slice(16 * half, 16 * half + 16)
                for gp in range(2):
                    ps_ = slice(64 * gp, 64 * gp + 64)
                    cs_ = slice(16 * gp, 16 * gp + 16)
                    tt(en_, xv_[ps_, :, sg_, 0, cs_], v3(w1)[ps_, qs_], v3(w2)[ps_, qs_], ALU.subtract, [n1, n2, nm], [nm + "_%d_%d" % (sg_, gp)])
                    tt(en_, xv_[ps_, :, sg_, 1, cs_], v3(w3)[ps_, qs_], v3(w4)[ps_, qs_], ALU.add, [n3, n4, nm], [nm + "_%d_%d" % (sg_, gp)])
    Wmb = xs[:, 0, :].rearrange("p (h s) -> p h s", h=8)
    tt("dve", Wmb, Ws32.rearrange("p (h s) -> p h s", h=8), tril32.unsqueeze(1).broadcast_to([128, 8, 128]),
       ALU.mult, ["Ws32", "tril"], ["xs0"])
    for h in range(8):
        tr(PT[:, h // 4, (h % 4) * 128:(h % 4 + 1) * 128], Wmb[:, h, :], ["xs0"], ["PT%d" % (h // 4)])
    for kh in range(2):
        cp("dve", WmT[:, 4 * kh:4 * kh + 4, :], PT[:, kh, 0:512].rearrange("p (h t) -> p h t", h=4), ["PT%d" % kh], ["WmT"])
    for kh in range(2):
        mm(PS[:, kh, :], onesb[:], WmT[:, 4 * kh:4 * kh + 4, :], True, True, ["ones", "WmT"], ["ps%d" % kh])
        for hh in range(4):
            h = 4 * kh + hh
            stt("dve", BiasT[:, h, :], PS[:, kh, hh * 128:(hh + 1) * 128], lnb[:, h:h + 1],
                bsB32.rearrange("p (h t) -> p h t", h=8)[:, h, :], ALU.mult, ALU.add,
                ["ps%d" % kh, "lnb", "bsB"], ["BiasT"])

    for j in range(8):
        bk = 2 + (j % 2)
        for i in range(4):
            q = 4 * j + i
            osl = PS[32 * i:32 * i + 32, bk, :]
            mm(osl[:, 0:32], BBD[:, q, 0, :], ZB0[:, q, 0, :], True, False, ["BBD", "ZB0"], ["ps%d" % bk], tp=(0, 32 * i), sgc=True)
            mm(osl[:, 0:32], BBD[:, q, 1, :], ZB0[:, q, 1, :], False, True, ["BBD", "ZB0"], ["ps%d" % bk], tp=(0, 32 * i), sgc=True)
            o2 = osl[:, 32:256].rearrange("p (k c) -> p k c", k=7)
            mm(o2, BBD[:, q, 0, :], WOUT[:, q, 0:7, 0, :], True, False, ["BBD", "WOUT"], ["ps%d" % bk], tp=(0, 32 * i), sgc=True)
            mm(o2, BBD[:, q, 1, :], WOUT[:, q, 0:7, 1, :], False, True, ["BBD", "WOUT"], ["ps%d" % bk], tp=(0, 32 * i), sgc=True)
        for i in range(4):
            sl_ = slice(32 * i, 32 * i + 32)
            cp("act" if i % 2 else "dve", TK[sl_, j, 1:8, sl_],
               PS[sl_, bk, 32:256].rearrange("p (k c) -> p k c", k=7), ["ps%d" % bk], ["TK"])
            stt("dve", TK[sl_, j, 0, sl_], idt32[sl_, sl_], Dl[sl_, j:j + 1], PS[sl_, bk, 0:32], ALU.mult, ALU.add,
                ["ps%d" % bk, "pp_id", "Dl"], ["TK"])
    cnt = 0
    for j in range(8):
        for g4 in range(4):
            bk = 2 + (cnt % 4)
            cnt += 1
            for s4 in range(4):
                sr = 4 * g4 + s4
                sg_, ri = sr // 2, sr % 2
                for i in range(4):
                    q = 4 * j + i
                    xv_, q0 = XBD(q, q + 1)
                    nm = "XBDlo" if q < 16 else "XBDhi"
                    mm(PS[32 * i:32 * i + 32, bk, s4 * 128:(s4 + 1) * 128], xv_[:, q - q0, sg_, ri, :], identb[:], True, True,
                       [nm, nm + "_%d_0" % sg_, nm + "_%d_1" % sg_, "ident"], ["ps%d" % bk], tp=(0, 32 * i))
            cp("act" if cnt % 2 else "dve", WIN[:, j, 4 * g4:4 * g4 + 4, :],
               PS[:, bk, :].rearrange("p (s c) -> p s c", s=4), ["ps%d" % bk], ["WIN"])
    memset("dve", hst[:], 0.0, ["hst"])
    dma("sp", gBf32[:], gBf, (), ["gBf"], "pp3", waitfor=["u5", "u6"])
    PREP_BUFS_A = []
    PREP_BUFS_B = []
    PREP_BUFS_H = []
    PREP_XV = []
    P.barrier()

    ring_use = [0]

    WBASE = {"w_in0": 0, "w_glu": 4, "w_out0": 6, "w_in1": 8, "w_out1": 14}
    cur_step = [0]

    preloaded = {}

    def preload_slab(W, s, step):
        old = cur_step[0]
        cur_step[0] = step
        preloaded[(W.tensor.name, s)] = load_slab(W, s)
        cur_step[0] = old

    def load_slab(W, s):
        if (W.tensor.name, s) in preloaded:
            return preloaded.pop((W.tensor.name, s))
        slot = ring_use[0] % 2
        ring_use[0] += 1
        wid = WBASE[W.tensor.name] + s
        if cur_step[0] == 0:
            dma("pool", ring[:, slot, :, :], W[:, 512 * s:512 * s + 512].rearrange("(k p) c -> p k c", p=128),
                (), ["ring%d" % slot], "ring%d" % slot)
            dma("sp", wscr[wid], ring[:, slot, :, :], ["ring%d" % slot], ["wscr%d" % wid], "wscr%d" % wid)
        else:
            dma("sp", ring[:, slot, :, :], wscr[wid], ["wscr%d" % wid], ["ring%d" % slot], "ring%d" % slot)
        return slot

    Astg = A[:, 0:4, :].rearrange("p a (b c) -> p (a b) c", c=512)
    ASTG_A = ["A%d_%d" % (j_, b_) for j_ in range(4) for b_ in range(2)]

    Astg0 = A[:, 4:8, :].rearrange("p a (b c) -> p (a b) c", c=512)
    ASTG_A0 = ["A%d_%d" % (j_, b_) for j_ in range(4, 8) for b_ in range(2)]

    def load_slab_stage(W, s, hi=False):
        wid = WBASE[W.tensor.name] + s
        dst, nm, key, wf = (Astg0, "Astage0", "astage0", ASTG_A0) if hi else (Astg, "Astage", "astage", ASTG_A)
        if cur_step[0] == 0:
            dma("pool", dst, W[:, 512 * s:512 * s + 512].rearrange("(k p) c -> p k c", p=128),
                (), [nm], key, waitfor=wf)
            dma("sp", wscr[wid], dst, [nm], ["wscr%d" % wid], "wscr%d" % wid)
        else:
            dma("sp", dst, wscr[wid], ["wscr%d" % wid], [nm], key, waitfor=wf)

    dbank = [0]

    def next_dbank():
        b_ = dbank[0] % 2
        dbank[0] += 1
        return b_

    sbank = [0]

    def next_sbank():
        b_ = 2 + (sbank[0] % 4)
        sbank[0] += 1
        return b_

    def hT_res(b):
        return ["hT%d_%d" % (m, kh) for m in range(4 * b, 4 * b + 4) for kh in range(2)]

    def rms_s1(m, gBb, so, tag):
        xm = xsub(m)
        xb = m % 2
        act(xs[:, xb, :], xm, AF.Square, [XVR[m]], ["xs%d" % xb], accum=stat[:, so + m:so + m + 1])
        act(stat[:, so + 8 + m:so + 9 + m], stat[:, so + m:so + m + 1], AF.Sqrt, ["xs%d" % xb], [tag + "a%d" % m],
            bias=epsc[:, 0:1], scale=1.0 / D)
        recip(stat[:, so + 16 + m:so + 17 + m], stat[:, so + 8 + m:so + 9 + m], [tag + "a%d" % m], [tag + "b%d" % m])
        stt("dve", xs[:, xb, :], xm, stat[:, so + 16 + m:so + 17 + m], gBb[:], ALU.mult, ALU.mult,
            [XVR[m], tag + "b%d" % m, "gB0", "gB1"], ["xs%d" % xb])

    def rms_s2(m):
        xb = m % 2
        for kh in range(2):
            for kk in range(4):
                k = 4 * kh + kk
                tr(PT[:, kh, kk * 128:(kk + 1) * 128], xs[:, xb, k * 128:(k + 1) * 128], ["xs%d" % xb], ["PT%d" % kh])
            cp("act" if kh == 0 else "dve", hT[:, 4 * kh:4 * kh + 4, m * 128:(m + 1) * 128],
               PT[:, kh, 0:512].rearrange("p (k t) -> p k t", k=4), ["PT%d" % kh],
               ["hT%d_%d" % (m, kh)], waitfor=["Hh0a", "Hh0b", "Hh1a", "Hh1b"])

    def rms_order():
        o = [("s1", 0), ("s1", 1)]
        for m in range(8):
            o.append(("s2", m))
            if m + 2 < 8:
                o.append(("s1", m + 2))
        return o

    def run_rms(order, ptr, upto_m, gBb, so, tag):
        while ptr < len(order) and order[ptr][1] <= upto_m:
            kind, m = order[ptr]
            if kind == "s1":
                rms_s1(m, gBb, so, tag)
            else:
                rms_s2(m)
            ptr += 1
        return ptr

    def load_x_m(n, m):
        b, blk = m // 4, m % 4
        dma("sp", xsub(m), x2[b, TS * n + 128 * blk:TS * n + 128 * (blk + 1), :], (), [XVR[m]], "xm%d" % m,
            waitfor=["Vh0", "Vh1"])

    def reload_x(n):
        for b in range(2):
            dma("sp", XV[:, 4096 * b:4096 * (b + 1)].rearrange("p (m d) -> p m d", m=4),
                x2[b, TS * n:TS * (n + 1), :].rearrange("(m p) d -> p m d", p=128),
                (), [XVR[m] for m in range(4 * b, 4 * b + 4)], "x%d" % b, waitfor=["Vh0", "Vh1"])

    def dense_fm(W, s_list, evac, src=None, b_outer=False, hooks=None):
        if b_outer:
            assert len(s_list) == 2 and src is None
            slots = [load_slab(W, s) for s in s_list]
            tix = 0
            for b in range(2):
                for si, s in enumerate(s_list):
                    for jj in range(4):
                        if hooks and tix in hooks:
                            hooks[tix]()
                        tix += 1
                        bk = next_dbank()
                        for k in range(8):
                            mm(PS[:, bk, :], ring[:, slots[si], k, jj * 128:(jj + 1) * 128], hT[:, k, b * 512:(b + 1) * 512],
                               k == 0, k == 7, ["ring%d" % slots[si]] + hT_res(b), ["ps%d" % bk])
                        evac(s, jj, b, bk)
            return
        for s in s_list:
            slot = load_slab(W, s)
            for jj in range(4):
                for b in range(2):
                    bk = next_dbank()
                    for k in range(8):
                        if src is None:
                            rhs, rres = hT[:, k, b * 512:(b + 1) * 512], hT_res(b)
                        else:
                            rhs, rres = A[:, k, b * 512:(b + 1) * 512], ["A%d_%d" % (k, b)]
                        mm(PS[:, bk, :], ring[:, slot, k, jj * 128:(jj + 1) * 128], rhs,
                           k == 0, k == 7, ["ring%d" % slot] + rres, ["ps%d" % bk])
                    evac(s, jj, b, bk)

    def tm_tiles(W, post_m, pre=None, stage_both=False):
        if stage_both:
            load_slab_stage(W, 0, hi=True)
            slot0 = None
        else:
            slot0 = load_slab(W, 0)
        load_slab_stage(W, 1)
        if pre is not None:
            pre()
        yield
        for m in range(8):
            b = m // 4
            for dh in range(2):
                bk = next_dbank()
                for k in range(8):
                    if dh == 0 and slot0 is None:
                        rhs, rres = Astg0[:, k, :], "Astage0"
                    elif dh == 0:
                        rhs, rres = ring[:, slot0, k, :], "ring%d" % slot0
                    else:
                        rhs, rres = Astg[:, k, :], "Astage"
                    mm(PS[:, bk, :], Bz[:, k, m * 128:(m + 1) * 128], rhs, k == 0, k == 7,
                       [rres, "B%d_%d" % (k, b)], ["ps%d" % bk])
                tt("dve", xsub(m)[:, dh * 512:(dh + 1) * 512], PS[:, bk, :], xsub(m)[:, dh * 512:(dh + 1) * 512], ALU.add,
                   ["ps%d" % bk, XVR[m]], [XVR[m]])
                if dh == 1 and m >= 1:
                    post_m(m - 1)
                yield
        post_m(7)

    def dense_tm_out(W, post_m):
        slots = [load_slab(W, 0), load_slab(W, 1)]
        for m in range(8):
            b = m // 4
            for dh in range(2):
                bk = next_dbank()
                for k in range(8):
                    mm(PS[:, bk, :], Bz[:, k, m * 128:(m + 1) * 128], ring[:, slots[dh], k, :], k == 0, k == 7,
                       ["ring%d" % slots[dh], "B%d_%d" % (k, b)], ["ps%d" % bk])
                tt("dve", xsub(m)[:, dh * 512:(dh + 1) * 512], PS[:, bk, :], xsub(m)[:, dh * 512:(dh + 1) * 512], ALU.add,
                   ["ps%d" % bk, XVR[m]], [XVR[m]])
            if m >= 1:
                post_m(m - 1)
        post_m(7)

    AR2r = ARI[:, 0, 0, :].unsqueeze(2).broadcast_to([128, 32, 2])
    ARn = ARI[:, 1, 0, :].unsqueeze(2).broadcast_to([128, 32, 2])
    ARp = ARI[:, 1, 1, :].unsqueeze(2).broadcast_to([128, 32, 2])

    def h4(a):
        return a.rearrange("p r (q b) -> p r q b", b=2)

    def h3(a):
        return a.rearrange("p (q b) -> p q b", b=2)

    Hall = hTf.rearrange("p (r q b c) -> p r q b c", r=2, q=32, b=2)
    Vall = XV[:, :].rearrange("p (r q b c) -> p r q b c", r=2, q=32, b=2)

    def ttn(out, in0, in1, op, reads, writes, waitfor=()):
        P.add("dve", lambda e, out=out, in0=in0, in1=in1, op=op: e.tensor_tensor(out=out, in0=in0, in1=in1, op=op),
              reads, writes, waitfor=waitfor, noself=True)

    L0_ORDER = rms_order()
    l0_ptr = 0
    for m in range(8):
        load_x_m(0, m)

    for n in range(NSTEP):
        cur_step[0] = n
        hooks = None
        if n > 0:
            pend = L0_ORDER[l0_ptr:]
            sched = {}
            slot_ = 2
            st = {"p": l0_ptr}

            def mk(upto_idx):
                def f():
                    while st["p"] <= upto_idx:
                        kind, m = L0_ORDER[st["p"]]
                        if kind == "s1":
                            rms_s1(m, gB0b, 0, "r0")
                        else:
                            rms_s2(m)
                        st["p"] += 1
                return f
            idxs = list(range(l0_ptr, len(L0_ORDER)))
            lead = l0_ptr
            while lead < len(L0_ORDER) and L0_ORDER[lead][0] == "s1":
                lead += 1
            mk(lead - 1)()
            rest = list(range(st["p"], len(L0_ORDER)))
            hooks = {}
            for r_i, li in enumerate(rest):
                hooks[min(7, 2 + 3 * r_i)] = mk(li)
            hooks[8] = mk(len(L0_ORDER) - 1)
            l0_ptr = len(L0_ORDER)
        else:
            l0_ptr = run_rms(L0_ORDER, l0_ptr, 7, gB0b, 0, "r0")

        def ev_in0(s, jj, b, bk):
            if s < 2:
                j = 4 * s + jj
                cp("dve", A[:, j, b * 512:(b + 1) * 512], PS[:, bk, :], ["ps%d" % bk], ["A%d_%d" % (j, b)], waitfor=["Astage", "Astage0"])
            else:
                j = 4 * (s - 2) + jj
                act(Bz[:, j, b * 512:(b + 1) * 512], PS[:, bk, :], AF.Silu, ["ps%d" % bk], ["B%d_%d" % (j, b)])
        dense_fm(w_in0, [0, 1], ev_in0, b_outer=True, hooks=hooks)

        for j in range(8):
            for ri in range(2):
                col = ((j % 2) * 2 + ri) * 128
                for sg_ in range(8):
                    for i in range(4):
                        mm(PS[:, 2 + i, col:col + 128], WIN[32 * i:32 * i + 32, j, 2 * sg_ + ri, :],
                           A[32 * i:32 * i + 32, j, :].rearrange("p (c t) -> p c t", t=8)[:, :, sg_],
                           sg_ == 0, sg_ == 7, ["WIN", "A%d_0" % j, "A%d_1" % j], ["ps%d" % (2 + i)], tp=(32 * i, 0))
            for i in range(4):
                q = 4 * j + i
                for ri in range(2):
                    col = ((j % 2) * 2 + ri) * 128
                    cp("act" if i % 2 else "dve", Vv(ri)[:, q, :, :],
                       PS[:, 2 + i, col:col + 128].rearrange("p (b c) -> p b c", b=2),
                       ["ps%d" % (2 + i)], [XVR[4 * ri + q // 8]])
        for ct in range(64):
            if ct == 0:
                hr_, hi_ = h3(hst[:, 0, :]), h3(hst[:, 1, :])
                hres = ["hst"]
            else:
                hr_, hi_ = Vall[:, 0, :, :, ct - 1], Vall[:, 1, :, :, ct - 1]
                hres = ["Vh%d" % ((ct - 1) // 32)]
            vres = "Vh%d" % (ct // 32)
            wf = XVR if ct == 0 else ()
            ttn(h3(rt[:, 0, 0, :]), hr_, AR2r, ALU.mult, hres + ["ARI"], ["rtA"], waitfor=wf)
            ttn(h3(rt[:, 0, 1, :]), hi_, AR2r, ALU.mult, hres + ["ARI"], ["rtB"])
            ttn(h3(rt[:, 1, 0, :]), hi_, ARn, ALU.mult, hres + ["ARI"], ["rtC"])
            ttn(h3(rt[:, 1, 1, :]), hr_, ARp, ALU.mult, hres + ["ARI"], ["rtD"])
            ttn(rt[:, 2, 0, :], rt[:, 0, 0, :], rt[:, 1, 0, :], ALU.add, ["rtA", "rtC"], ["rtE"])
            ttn(rt[:, 2, 1, :], rt[:, 0, 1, :], rt[:, 1, 1, :], ALU.add, ["rtB", "rtD"], ["rtF"])
            ttn(Vall[:, 0, :, :, ct], h3(rt[:, 2, 0, :]), Vall[:, 0, :, :, ct], ALU.add, ["rtE", vres], [vres])
            ttn(Vall[:, 1, :, :, ct], h3(rt[:, 2, 1, :]), Vall[:, 1, :, :, ct], ALU.add, ["rtF", vres], [vres])
        dense_fm(w_in0, [2, 3], ev_in0)

        glu_slots = []

        def glu_half(h2):
            if not glu_slots:
                glu_slots.extend([load_slab(w_glu, 0), load_slab(w_glu, 1)])
            cnt_ = 0
            for s_ in range(2):
                slot = glu_slots[s_]
                for jj in range(4):
                    j = 4 * s_ + jj
                    for b in range(2):
                        bk = next_dbank()
                        c0 = b * 512 + 256 * h2
                        for k in range(8):
                            mm(PS[:, bk, 0:256], ring[:, slot, k, jj * 128:(jj + 1) * 128], A[:, k, c0:c0 + 256],
                               k == 0, k == 7, ["ring%d" % slot, "A%d_%d" % (k, b)], ["ps%d" % bk])
                        eb = cnt_ % 2
                        cnt_ += 1
                        act(evb[:, eb, 0:256], PS[:, bk, 0:256], AF.Sigmoid, ["ps%d" % bk, "bglu"], ["evb%d" % eb],
                            bias=bglu[:, j:j + 1])
                        tt("pool" if h2 == 0 else "dve", Bz[:, j, c0:c0 + 256], evb[:, eb, 0:256], Bz[:, j, c0:c0 + 256], ALU.mult,
                           ["evb%d" % eb, "B%d_%d" % (j, b)], ["B%d_%d" % (j, b)])

        for h2 in range(2):
            if h2 == 1:
                glu_half(0)
            if h2 == 0:
                cp("act", Hall[:, :, :, :, 0], h4(hst[:, 0:2, :]), ["hst"], HBR + ["Hh0a", "Hh0b"])
                cp("act", Hall[:, :, 0:16, :, 1:32], Vall[:, :, 0:16, :, 0:31], ["Vh0"], HBR + ["Hh0a"])
                cp("act", Hall[:, :, 16:32, :, 1:32], Vall[:, :, 16:32, :, 0:31], ["Vh0"], HBR + ["Hh0b"])
            else:
                cp("act", Hall[:, :, 0:16, :, 32:64], Vall[:, :, 0:16, :, 31:63], ["Vh0", "Vh1"], HBR + ["Hh1a"])
                cp("act", Hall[:, :, 16:32, :, 32:64], Vall[:, :, 16:32, :, 31:63], ["Vh0", "Vh1"], HBR + ["Hh1b"])
                cp("dve", h4(hst[:, 0:2, :]), Vall[:, :, :, :, 63], ["Vh1"], ["hst"])
            for jg in range(2):
                bks = {}
                for j in range(4 * jg, 4 * jg + 4):
                    bk = next_sbank()
                    bks[j] = bk
                    ures = ["A%d_0" % j, "A%d_1" % j]
                    uv = A[:, j, :].rearrange("p (b c t) -> p b c t", b=2, t=8)[:, :, 32 * h2:32 * h2 + 32, :]
                    ov = PS[:, bk, :].rearrange("p (b c t) -> p b c t", b=2, t=8)
                    for k in range(8):
                        mm(ov[:, :, :, k:8], TK[:, j, k, :], uv[:, :, :, 0:8 - k], k == 0, False, ["TK"] + ures, ["ps%d" % bk], sgc=True)
                for j in range(4 * jg, 4 * jg + 4):
                    bk = bks[j]
                    ures = ["A%d_0" % j, "A%d_1" % j]
                    for tau in range(8):
                        for ri in range(2):
                            for i in range(4):
                                q = 4 * j + i
                                last = (i == 3 and tau == 7 and ri == 1)
                                mm(PS[32 * i:32 * i + 32, bk, :].rearrange("p (b c t) -> p b c t", b=2, t=8)[:, :, :, tau],
                                   WOUT[:, q, tau, ri, :], Hv(ri)[:, q, :, 32 * h2:32 * h2 + 32], False, last,
                                   ["WOUT", "Hh%d%s" % (h2, "a" if q < 16 else "b")], ["ps%d" % bk], tp=(0, 32 * i), sgc=True)
                    av_ = A[:, j, :].rearrange("p (b x) -> p b x", b=2)[:, :, 256 * h2:256 * h2 + 256]
                    bv_ = Bz[:, j, :].rearrange("p (b x) -> p b x", b=2)[:, :, 256 * h2:256 * h2 + 256]
                    act(av_, PS[:, bk, :].rearrange("p (b x) -> p b x", b=2), AF.Gelu, ["ps%d" % bk], ures)
                    tt("pool" if h2 == 0 else "dve", bv_, bv_, av_, ALU.mult, ures + ["B%d_0" % j, "B%d_1" % j],
                       ["B%d_0" % j, "B%d_1" % j])
        glu_half(1)
        reload_x(n)
        L1_ORDER = rms_order()
        l1 = [0]

        def post0(m):
            l1[0] = run_rms(L1_ORDER, l1[0], m, gB1b, 0, "r1")
        for _ in tm_tiles(w_out0, post0, pre=lambda n=n: preload_slab(w_in1, 0, n)):
            pass
        l1[0] = run_rms(L1_ORDER, l1[0], 7, gB1b, 0, "r1")

        def ev_in1(s, jj, b, bk):
            if s < 2:
                j = 4 * s + jj
                act(A[:, j, b * 512:(b + 1) * 512], PS[:, bk, :], AF.Gelu, ["ps%d" % bk], ["A%d_%d" % (j, b)], waitfor=["Astage", "Astage0"])
            else:
                j = 4 * (s - 4) + jj
                act(Bz[:, j, b * 512:(b + 1) * 512], PS[:, bk, :], AF.Silu, ["ps%d" % bk], ["B%d_%d" % (j, b)])
                tt("dve", Bz[:, j, b * 512:(b + 1) * 512], Bz[:, j, b * 512:(b + 1) * 512], A[:, j, b * 512:(b + 1) * 512], ALU.mult,
                   ["B%d_%d" % (j, b), "A%d_%d" % (j, b)], ["B%d_%d" % (j, b)])
        dense_fm(w_in1, [0, 1, 4, 5], ev_in1)
        vslots = [load_slab(w_in1, 2), load_slab(w_in1, 3)]
        vb = [0]
        for m in range(8):
            bks = []
            for half in range(2):
                bk = [0, 1, 4, 5][vb[0] % 4]
                vb[0] += 1
                bks.append(bk)
                for k in range(8):
                    mm(PS[:, bk, :], hT[:, k, m * 128:(m + 1) * 128], ring[:, vslots[half], k, :], k == 0, k == 7,
                       ["ring%d" % vslots[half], "hT%d_%d" % (m, k // 4)], ["ps%d" % bk])
            for half in range(2):
                bk = bks[half]
                gdst = hT[:, 4 * half:4 * half + 4, m * 128:(m + 1) * 128]
                act(gdst, PS[:, bk, :].rearrange("p (h c) -> p h c", h=4), AF.Gelu, ["ps%d" % bk], ["hT%d_%d" % (m, half)],
                    accum=S12[:, m, 0, half:half + 1])
                stt("dve", evb[:, half, :].rearrange("p (h c) -> p h c", h=4), gdst, 1.0, gdst, ALU.mult, ALU.mult,
                    ["hT%d_%d" % (m, half)], ["evb%d" % half], accum=S12[:, m, 1, half:half + 1])
            if m % 4 == 3:
                g_ = m // 4
                so = 48 + 24 * g_
                ms_ = slice(4 * g_, 4 * g_ + 4)
                S12v = S12[:].rearrange("p m w h -> p w m h")
                tg = "ln%d" % g_
                tt("dve", stat[:, so:so + 8].rearrange("p (w m) -> p w m", w=2), S12v[:, :, ms_, 0], S12v[:, :, ms_, 1], ALU.add,
                   ["evb0", "evb1"] + HBR, [tg + "A"])
                ts("dve", stat[:, so:so + 8], stat[:, so:so + 8], 1.0 / D, None, ALU.mult, None, [tg + "A"], [tg + "A"])
                tt("dve", stat[:, so + 8:so + 12], stat[:, so:so + 4], stat[:, so:so + 4], ALU.mult, [tg + "A"], [tg + "B"])
                tt("dve", stat[:, so + 12:so + 16], stat[:, so + 4:so + 8], stat[:, so + 8:so + 12], ALU.subtract,
                   [tg + "A", tg + "B"], [tg + "C"])
                act(stat[:, so + 16:so + 20], stat[:, so + 12:so + 16], AF.Sqrt, [tg + "C"], [tg + "D"], bias=epsc[:, 1:2])
                recip(stat[:, so + 20:so + 24], stat[:, so + 16:so + 20], [tg + "D"], [tg + "E"])
                for mm_ in range(4 * g_, 4 * g_ + 4):
                    blkv = hT[:, :, mm_ * 128:(mm_ + 1) * 128]
                    ts("dve", blkv, blkv, stat[:, so + (mm_ % 4):so + (mm_ % 4) + 1], stat[:, so + 20 + (mm_ % 4):so + 21 + (mm_ % 4)],
                       ALU.subtract, ALU.mult,
                       ["hT%d_0" % mm_, "hT%d_1" % mm_, tg + "A", tg + "E"], ["hT%d_0" % mm_, "hT%d_1" % mm_])
        def mix_tile(h, b, idx):
            bk = next_sbank()
            for blk in range(4):
                m = 4 * b + blk
                mm(PS[:, bk, blk * 128:(blk + 1) * 128], hT[:, h, m * 128:(m + 1) * 128], WmT[:, h, :], True, True,
                   ["hT%d_%d" % (m, h // 4), "WmT"], ["ps%d" % bk])
            eb = idx % 2
            stt("dve", ev32[:, eb, :].rearrange("p (k t) -> p k t", k=4), PS[:, bk, :].rearrange("p (k t) -> p k t", k=4),
                lng[:, h:h + 1], BiasT[:, h, :].unsqueeze(1).broadcast_to([128, 4, 128]), ALU.mult, ALU.add,
                ["ps%d" % bk, "lng", "BiasT"], ["ev32_%d" % eb])
            tt("pool" if idx % 2 else "dve", Bz[:, h, b * 512:(b + 1) * 512], ev32[:, eb, :], Bz[:, h, b * 512:(b + 1) * 512], ALU.mult,
               ["ev32_%d" % eb, "B%d_%d" % (h, b)], ["B%d_%d" % (h, b)])
        l0_ptr = 0
        nxt = [0]

        def post1(m, n=n):
            xm = xsub(m)
            xb = m % 2
            fo = 24
            act(evb[:, xb, :].rearrange("p (a c) -> p a c", a=1)[:, 0, :] if False else ev32[:, xb, :], xm[:, 0:512], AF.Square,
                [XVR[m]], ["ev32_%d" % xb], accum=stat[:, fo + m:fo + m + 1])
            act(evb[:, xb, :], xm[:, 512:1024], AF.Square, [XVR[m]], ["evb%d" % xb], accum=stat[:, fo + 8 + m:fo + 9 + m])
            tt("dve", stat[:, fo + m:fo + m + 1], stat[:, fo + m:fo + m + 1], stat[:, fo + 8 + m:fo + 9 + m], ALU.add,
               ["ev32_%d" % xb, "evb%d" % xb], ["fa%d" % m])
            act(stat[:, fo + 8 + m:fo + 9 + m], stat[:, fo + m:fo + m + 1], AF.Sqrt, ["fa%d" % m], ["fb%d" % m],
                bias=epsc[:, 0:1], scale=1.0 / D)
            recip(stat[:, fo + 16 + m:fo + 17 + m], stat[:, fo + 8 + m:fo + 9 + m], ["fb%d" % m], ["fc%d" % m])
            tt("pool", xm, xm, gBf32[:], ALU.mult, [XVR[m], "gBf", "ev32_%d" % xb, "evb%d" % xb], [XVR[m]])
            act(xm, xm, AF.Copy, [XVR[m], "fc%d" % m], [XVR[m]], scale=stat[:, fo + 16 + m:fo + 17 + m])
            b, blk = m // 4, m % 4
            dma("sp", y2[b, TS * n + 128 * blk:TS * n + 128 * (blk + 1), :], xm, [XVR[m]], ["out%d" % m], "out%d" % m)
            if n + 1 < NSTEP:
                load_x_m(n + 1, m)
                nxt[0] = run_rms(L0_ORDER, nxt[0], m - 3, gB0b, 0, "r0")
        def pre1(n=n):
            preload_slab(w_in0, 0, n + 1)
            preload_slab(w_in0, 1, n + 1)
        g_out = tm_tiles(w_out1, post1, pre=pre1 if n + 1 < NSTEP else None, stage_both=True)
        next(g_out)
        for h in range(8):
            mix_tile(h, 0, h)
        for h in range(8):
            next(g_out)
            mix_tile(h, 1, 8 + h)
        for _ in g_out:
            pass
        l0_ptr = nxt[0]
    P.add("sp", None, ["out%d" % m for m in range(8)], ())

    P.finalize()
    esem = {e: es.enter_context(nc.semaphore("sem_" + e)) for e in Prog.ENG}
    dsem = {k: es.enter_context(nc.semaphore("dsem_" + k)) for k in P.dmacnt}
    block = es.enter_context(nc.Block())

    @block.tensor
    def _(e):
        P.emit_engine("pe", e, esem, dsem)

    @block.scalar
    def _(e):
        P.emit_engine("act", e, esem, dsem)

    @block.vector
    def _(e):
        P.emit_engine("dve", e, esem, dsem)

    @block.gpsimd
    def _(e):
        P.emit_engine("pool", e, esem, dsem)

    @block.sync
    def _(e):
        P.emit_engine("sp", e, esem, dsem)

    es.close()
    return nc


def host_inputs(inp):
    f = lambda a: np.ascontiguousarray(np.asarray(a, dtype=np.float32))
    shared = {}
    shared["gB0"] = f(np.broadcast_to(inp["norm_g"][0][None, :], (128, D)))
    shared["gB1"] = f(np.broadcast_to(inp["norm_g"][1][None, :], (128, D)))
    shared["gBf"] = f(np.broadcast_to(inp["final_g"][None, :], (128, D)))
    shared["w_in0"] = f(inp["s5_w_in"][0])
    shared["w_glu"] = f(inp["s5_w_glu"][0])
    shared["w_out0"] = f(inp["s5_w_out"][0])
    shared["w_in1"] = f(inp["sgu_w_in"][0])
    shared["w_out1"] = f(inp["sgu_w_out"][0])

    def gp_layout(a):
        a = np.asarray(a)
        rest = a.shape[2:]
        a = a.reshape((32, 2, 64) + rest)
        a = np.moveaxis(a, 0, 2)
        return a.reshape((128, 32) + rest)
    shared["Are"] = f(gp_layout(inp["s5_A_re"][0]))
    shared["Aim"] = f(gp_layout(inp["s5_A_im"][0]))
    shared["Ldt"] = f(gp_layout(np.broadcast_to(np.asarray(inp["s5_log_dt"][0])[:, None], (64, 64))))
    shared["Bre"] = f(gp_layout(inp["s5_B_re"][0]).reshape(128, 512))
    shared["Bim"] = f(gp_layout(inp["s5_B_im"][0]).reshape(128, 512))
    shared["Cre"] = f(gp_layout(np.transpose(inp["s5_C_re"][0], (0, 2, 1))).reshape(128, 512))
    shared["Cim"] = f(gp_layout(np.transpose(inp["s5_C_im"][0], (0, 2, 1))).reshape(128, 512))
    shared["Dl"] = f(np.asarray(inp["s5_D"][0]).reshape(8, 128).T)
    shared["bglu"] = f(np.asarray(inp["s5_b_glu"][0]).reshape(8, 128).T)
    shared["Ws"] = f(np.transpose(inp["sgu_w_s"][0], (1, 0, 2)).reshape(128, 1024))
    shared["tril"] = f(np.tril(np.ones((128, 128), np.float32)))
    shared["lng"] = f(np.asarray(inp["sgu_ln_g"][0]).reshape(8, 128).T)
    shared["lnb"] = f(np.asarray(inp["sgu_ln_b"][0]).reshape(8, 128).T)
    shared["bsB"] = f(np.broadcast_to(np.asarray(inp["sgu_b_s"][0]).reshape(1, 1024), (128, 1024)))
    shared["ident"] = f(np.eye(128, dtype=np.float32))
    return shared


_NC_CACHE = {}


def kernel(**inputs):
    x = np.asarray(inputs["x"], dtype=np.float32)
    shared = host_inputs(inputs)
    if "nc" not in _NC_CACHE:
        _NC_CACHE["nc"] = build_nc()
    nc = _NC_CACHE["nc"]
    in_maps = []
    for c in range(NCORES):
        m = dict(shared)
        m["x2"] = np.ascontiguousarray(x[2 * c:2 * c + 2])
        in_maps.append(m)
    res = run_bass_kernel_spmd(nc, in_maps, core_ids=list(range(NCORES)))
    out = np.concatenate([np.asarray(r["y2"]) for r in res.results], axis=0)
    return out.astype(np.float32)
```

```python
import math
from contextlib import ExitStack
import numpy as np
import concourse.bass as bass
import concourse.mybir as mybir
from concourse.bass_utils import run_bass_kernel_spmd

F32 = mybir.dt.float32
BF16 = mybir.dt.bfloat16
AF = mybir.ActivationFunctionType
ALU = mybir.AluOpType

NCORES = 8
SEQ = 2048
D = 1024
NSTEP = 4
TS = 512
PI = math.pi


class Prog:
    ENG = ["pe", "act", "dve", "pool", "sp"]

    def __init__(self):
        self.ops = {e: [] for e in self.ENG}
        self.lastw = {}
        self.rd = {}
        self.dmacnt = {}

    def add(self, eng, fn, reads=(), writes=(), dma=None, waitfor=(), noself=False):
        deps = []
        for r in waitfor:
            w = self.lastw.get(r)
            if w is not None:
                deps.append(w)
            for e in self.rd.get(r, ()):
                if e[0] == "E" and e[1] == eng:
                    continue
                deps.append(e)
        for r in reads:
            w = self.lastw.get(r)
            if w is not None:
                deps.append(w)
        for r in writes:
            w = self.lastw.get(r)
            if w is not None:
                deps.append(w)
            for e in self.rd.get(r, ()):
                if e[0] == "E" and e[1] == eng:
                    continue
                deps.append(e)
        idx = len(self.ops[eng])
        if dma is not None:
            c = self.dmacnt.get(dma, 0) + 1
            self.dmacnt[dma] = c
            ev = ("D", dma, c)
        else:
            ev = ("E", eng, idx)
        keep = []
        seen = set()
        for d in deps:
            if d in seen:
                continue
            seen.add(d)
            if d[0] == "E" and d[1] == eng and (eng in ("pe", "sp") or noself):
                continue
            keep.append(d)
        self.ops[eng].append({"fn": fn, "deps": keep, "ev": ev, "signal": False})
        for r in reads:
            lst = self.rd.setdefault(r, [])
            if ev[0] == "E":
                lst[:] = [x for x in lst if not (x[0] == "E" and x[1] == eng)]
            else:
                lst[:] = [x for x in lst if not (x[0] == "D" and x[1] == ev[1])]
            lst.append(ev)
        for r in writes:
            self.lastw[r] = ev
            self.rd[r] = []
        return ev

    def barrier(self):
        lasts = []
        for e in self.ENG:
            for i in range(len(self.ops[e]) - 1, -1, -1):
                o = self.ops[e][i]
                if o["fn"] is not None and o["ev"][0] == "E":
                    lasts.append(o["ev"])
                    break
        dm = [("D", k, c) for k, c in self.dmacnt.items()]
        for e in self.ENG:
            deps = [d for d in lasts if d[1] != e] + dm
            self.ops[e].append({"fn": None, "deps": deps, "ev": ("E", e, len(self.ops[e])), "signal": False})
        self.lastw = {}
        self.rd = {}

    def finalize(self):
        for e in self.ENG:
            for o in self.ops[e]:
                for d in o["deps"]:
                    if d[0] == "E":
                        self.ops[d[1]][d[2]]["signal"] = True
        self.sigval = {}
        for e in self.ENG:
            c = 0
            for i, o in enumerate(self.ops[e]):
                if o["signal"]:
                    c += 1
                    self.sigval[(e, i)] = c

    def emit_engine(self, e, eng, esem, dsem):
        waited = {}
        for o in self.ops[e]:
            need = {}
            for d in o["deps"]:
                if d[0] == "E":
                    key = ("E", d[1])
                    val = self.sigval[(d[1], d[2])]
                else:
                    key = ("D", d[1])
                    val = 16 * d[2]
                if val > need.get(key, 0):
                    need[key] = val
            for key, val in need.items():
                if waited.get(key, 0) < val:
                    sem = esem[key[1]] if key[0] == "E" else dsem[key[1]]
                    eng.wait_ge(sem, val)
                    waited[key] = val
            if o["fn"] is None:
                continue
            ins = o["fn"](eng)
            if o["ev"][0] == "D":
                ins.then_inc(dsem[o["ev"][1]], 16)
            elif o["signal"]:
                ins.then_inc(esem[e], 1)


def build_nc(debug=False):
    nc = bass.Bass("TRN2", target_bir_lowering=False)
    P = Prog()

    def din(name, shape):
        return nc.dram_tensor(name, list(shape), F32, kind="ExternalInput").ap()

    x2 = din("x2", [2, SEQ, D])
    gB0 = din("gB0", [128, D])
    gB1 = din("gB1", [128, D])
    gBf = din("gBf", [128, D])
    w_in0 = din("w_in0", [D, 2 * D])
    w_glu = din("w_glu", [D, D])
    w_out0 = din("w_out0", [D, D])
    w_in1 = din("w_in1", [D, 3 * D])
    w_out1 = din("w_out1", [D, D])
    dAre = din("Are", [128, 32])
    dAim = din("Aim", [128, 32])
    dLdt = din("Ldt", [128, 32])
    dBre = din("Bre", [128, 512])
    dBim = din("Bim", [128, 512])
    dCre = din("Cre", [128, 512])
    dCim = din("Cim", [128, 512])
    dDl = din("Dl", [128, 8])
    dbglu = din("bglu", [128, 8])
    dWs = din("Ws", [128, 1024])
    dtril = din("tril", [128, 128])
    dlng = din("lng", [128, 8])
    dlnb = din("lnb", [128, 8])
    dbsB = din("bsB", [128, 1024])
    dident = din("ident", [128, 128])
    y2 = nc.dram_tensor("y2", [2, SEQ, D], F32, kind="ExternalOutput").ap()
    wscr = nc.dram_tensor("wscr", [16, 128, 8, 512], BF16, kind="Internal").ap()

    es = ExitStack()

    def sb(name, shape, dt):
        return es.enter_context(nc.sbuf_tensor("sb_" + name, list(shape), dt))

    XV = sb("XV", [128, 8192], F32)
    hT = sb("hT", [128, 8, 1024], BF16)
    A = sb("A", [128, 8, 1024], BF16)
    Bz = sb("Bz", [128, 8, 1024], BF16)
    TK = sb("TK", [128, 8, 8, 128], BF16)
    WIN = sb("WIN", [128, 8, 16, 128], BF16)
    WOUT = sb("WOUT", [128, 32, 8, 2, 32], BF16)
    ARI = sb("ARI", [128, 2, 2, 32], F32)
    ring = sb("ring", [128, 2, 8, 512], BF16)
    identb = sb("identb", [128, 128], BF16)
    onesb = sb("onesb", [128, 128], BF16)
    gB0b = sb("gB0b", [128, D], BF16)
    gB1b = sb("gB1b", [128, D], BF16)
    gBf32 = sb("gBf32", [128, D], F32)
    Dl = sb("Dl", [128, 8], F32)
    bglu = sb("bglu", [128, 8], F32)
    lng = sb("lng", [128, 8], F32)
    lnb = sb("lnb", [128, 8], F32)
    BiasT = sb("BiasT", [128, 8, 128], F32)
    WmT = sb("WmT", [128, 8, 128], BF16)
    xs = sb("xs", [128, 2, 1024], BF16)
    stat = sb("stat", [128, 112], F32)
    S12 = sb("S12", [128, 8, 2, 2], F32)
    hst = sb("hst", [128, 3, 64], F32)
    rt = sb("rt", [128, 3, 2, 64], F32)
    ev32 = sb("ev32", [128, 2, 512], F32)
    evb = sb("evb", [128, 2, 512], BF16)
    epsc = sb("epsc", [128, 2], F32)
    misc32 = sb("misc32", [128, 704], F32)
    PS = es.enter_context(nc.psum_tensor("PS", [128, 6, 512], F32))
    PT = es.enter_context(nc.psum_tensor("PT", [128, 2, 1024], BF16))

    def dma(q, out, in_, reads, writes, key, waitfor=()):
        P.add(q, lambda e, out=out, in_=in_: e.dma_start(out=out, in_=in_), reads, writes, dma=key, waitfor=waitfor)

    def mm(out, lhsT, rhs, start, stop, reads, writes, tp=None, sgc=False):
        def fn(e, out=out, lhsT=lhsT, rhs=rhs, start=start, stop=stop, tp=tp, sgc=sgc):
            kw = {}
            if tp is not None:
                kw["tile_position"] = tp
            if sgc:
                kw["skip_group_check"] = True
            return e.matmul(out, lhsT=lhsT, rhs=rhs, start=start, stop=stop, **kw)
        P.add("pe", fn, reads, writes)

    def tr(out, in_, reads, writes):
        P.add("pe", lambda e, out=out, in_=in_: e.transpose(out, in_, identb[:]), list(reads) + ["ident"], writes)

    def act(out, in_, func, reads, writes, bias=None, scale=None, accum=None, waitfor=()):
        def fn(e, out=out, in_=in_, func=func, bias=bias, scale=scale, accum=accum):
            kw = {}
            if bias is not None:
                kw["bias"] = bias
            if scale is not None:
                kw["scale"] = scale
            if accum is not None:
                kw["accum_out"] = accum
            return e.activation(out=out, in_=in_, func=func, **kw)
        P.add("act", fn, reads, writes, waitfor=waitfor)

    def cp(eng, out, in_, reads, writes, waitfor=()):
        if eng == "act":
            act(out, in_, AF.Copy, reads, writes, waitfor=waitfor)
        else:
            P.add(eng, lambda e, out=out, in_=in_: e.tensor_copy(out=out, in_=in_), reads, writes, waitfor=waitfor)

    def tt(eng, out, in0, in1, op, reads, writes, waitfor=()):
        P.add(eng, lambda e, out=out, in0=in0, in1=in1, op=op: e.tensor_tensor(out=out, in0=in0, in1=in1, op=op), reads, writes,
              waitfor=waitfor)

    def ts(eng, out, in0, s1, s2, op0, op1, reads, writes):
        def fn(e, out=out, in0=in0, s1=s1, s2=s2, op0=op0, op1=op1):
            if op1 is None:
                return e.tensor_scalar(out=out, in0=in0, scalar1=s1, scalar2=None, op0=op0)
            return e.tensor_scalar(out=out, in0=in0, scalar1=s1, scalar2=s2, op0=op0, op1=op1)
        P.add(eng, fn, reads, writes)

    def stt(eng, out, in0, scalar, in1, op0, op1, reads, writes, accum=None):
        def fn(e, out=out, in0=in0, scalar=scalar, in1=in1, op0=op0, op1=op1, accum=accum):
            if accum is None:
                return e.scalar_tensor_tensor(out=out, in0=in0, scalar=scalar, in1=in1, op0=op0, op1=op1)
            return e.scalar_tensor_tensor(out=out, in0=in0, scalar=scalar, in1=in1, op0=op0, op1=op1, accum_out=accum)
        P.add(eng, fn, reads, writes)

    def memset(eng, ap, val, writes):
        P.add(eng, lambda e, ap=ap, val=val: e.memset(ap, val), (), writes)

    def recip(out, in_, reads, writes):
        P.add("dve", lambda e, out=out, in_=in_: e.reciprocal(out=out, in_=in_), reads, writes)

    XVR = ["XV%d" % r for r in range(8)]
    HBR = ["hT%d_%d" % (m_, kh_) for m_ in range(8) for kh_ in range(2)]
    hTf = hT[:].rearrange("p k f -> p (k f)")

    def xsub(m):
        return XV[:, m * 1024:(m + 1) * 1024]

    def Vv(ri):
        return XV[:, ri * 4096:(ri + 1) * 4096].rearrange("p (q b c) -> p q b c", q=32, b=2)

    def Hv(ri):
        return hTf[:, ri * 4096:(ri + 1) * 4096].rearrange("p (q b c) -> p q b c", q=32, b=2)

    t32 = XV
    off = [0]

    def t32a(n):
        a = t32[:, off[0]:off[0] + n]
        off[0] += n
        return a

    offb = [0]

    def t32b(n):
        a = misc32[:, offb[0]:offb[0] + n]
        offb[0] += n
        return a

    idt32 = t32b(128)
    dma("sp", idt32, dident, (), ["pp_id"], "pp0")
    Are = t32b(32); Aim = t32b(32); Ldt = t32b(32)
    Bre = t32a(512); Bim = t32a(512); Cre = t32a(512); Cim = t32a(512)
    for i_, (t_, d_, nm) in enumerate([(Are, dAre, "Are"), (Aim, dAim, "Aim"), (Ldt, dLdt, "Ldt"), (Bre, dBre, "Bre"),
                                       (Bim, dBim, "Bim"), (Cre, dCre, "Cre"), (Cim, dCim, "Cim")]):
        dma("sp", t_, d_, (), [nm], "pp8_%d" % i_)
    Ws32 = ev32[:].rearrange("p a c -> p (a c)")
    tril32 = t32b(128)
    bsB32 = t32a(1024)
    dma("sp", Ws32, dWs, (), ["Ws32"], "pp5")
    dma("sp", tril32, dtril, (), ["tril"], "pp6")
    dma("sp", bsB32, dbsB, (), ["bsB"], "pp7")
    cp("act", identb[:], idt32, ["pp_id"], ["ident"])
    memset("dve", onesb[:], 1.0, ["ones"])
    memset("dve", epsc[:, 0:1], 1e-6, ["eps"])
    memset("dve", epsc[:, 1:2], 1e-5, ["eps"])
    dma("pool", gB0b[:], gB0, (), ["gB0"], "pp1")
    dma("pool", gB1b[:], gB1, (), ["gB1"], "pp2")
    for i_, (t_, d_, nm) in enumerate([(Dl, dDl, "Dl"), (bglu, dbglu, "bglu"), (lng, dlng, "lng"), (lnb, dlnb, "lnb")]):
        dma("sp", t_[:], d_, (), [nm], "pp4_%d" % i_)
    dtt = t32b(32); lre = t32b(32); xr = t32b(32); xi = t32b(32)
    act(dtt, Ldt, AF.Exp, ["Ldt"], ["dt"])
    ts("dve", lre, Are, -1e-4, None, ALU.min, None, ["Are"], ["lre"])
    tt("dve", xr, lre, dtt, ALU.mult, ["lre", "dt"], ["xr"])
    tt("dve", xi, Aim, dtt, ALU.mult, ["Aim", "dt"], ["xi"])
    TH = t32a(512); SC = t32a(512); MAG = t32a(256); Et = t32a(576)
    THv = TH.rearrange("p (e s q) -> p e s q", e=8, s=2)
    SCv = SC.rearrange("p (e s q) -> p e s q", e=8, s=2)
    MAGv = MAG.rearrange("p (e q) -> p e q", e=8)
    Etv = Et.rearrange("p (e r q) -> p e r q", e=9, r=2)
    for e_ in range(8):
        ts("dve", THv[:, e_, 0, :], xi, float(e_ + 1), None, ALU.mult, None, ["xi"], ["TH"])
        ts("dve", THv[:, e_, 1, :], xi, float(e_ + 1), 0.5 * PI, ALU.mult, ALU.add, ["xi"], ["TH"])
    MAGIC = 12582912.0
    ts("dve", SC, TH, 1.0 / (2 * PI), MAGIC, ALU.mult, ALU.add, ["TH"], ["SC"])
    ts("dve", SC, SC, -MAGIC, None, ALU.add, None, ["SC"], ["SC"])
    stt("dve", TH, SC, -2 * PI, TH, ALU.mult, ALU.add, ["SC", "TH"], ["TH"])
    ts("dve", TH, TH, -PI, PI, ALU.max, ALU.min, ["TH"], ["TH"])
    for e_ in range(8):
        act(MAGv[:, e_, :], xr, AF.Exp, ["xr"], ["MAG"], scale=float(e_ + 1))
    act(SC, TH, AF.Sin, ["TH"], ["SC"])
    memset("dve", Etv[:, 0, 0, :], 1.0, ["Et"])
    memset("dve", Etv[:, 0, 1, :], 0.0, ["Et"])
    tt("dve", Etv[:, 1:9, 0, :], MAGv, SCv[:, :, 1, :], ALU.mult, ["MAG", "SC"], ["Et"])
    tt("dve", Etv[:, 1:9, 1, :], MAGv, SCv[:, :, 0, :], ALU.mult, ["MAG", "SC"], ["Et"])
    cp("dve", ARI[:, 0, 0, :], Etv[:, 8, 0, :], ["Et"], ["ARI"])
    cp("dve", ARI[:, 0, 1, :], Etv[:, 8, 0, :], ["Et"], ["ARI"])
    ts("dve", ARI[:, 1, 0, :], Etv[:, 8, 1, :], -1.0, None, ALU.mult, None, ["Et"], ["ARI"])
    cp("dve", ARI[:, 1, 1, :], Etv[:, 8, 1, :], ["Et"], ["ARI"])
    nr = t32b(32); den = t32b(32); tA = t32b(32); tB = t32b(32); rr = t32b(32); rim = t32b(32)
    ni = Etv[:, 1, 1, :]
    ts("dve", nr, Etv[:, 1, 0, :], -1.0, None, ALU.add, None, ["Et"], ["nr"])
    tt("dve", den, lre, lre, ALU.mult, ["lre"], ["den"])
    tt("dve", tA, Aim, Aim, ALU.mult, ["Aim"], ["tA"])
    tt("dve", den, den, tA, ALU.add, ["den", "tA"], ["den"])
    recip(den, den, ["den"], ["den"])
    tt("dve", tA, nr, lre, ALU.mult, ["nr", "lre"], ["tA"])
    tt("dve", tB, ni, Aim, ALU.mult, ["Et", "Aim"], ["tB"])
    tt("dve", tA, tA, tB, ALU.add, ["tA", "tB"], ["tA"])
    tt("dve", rr, tA, den, ALU.mult, ["tA", "den"], ["rr"])
    tt("dve", tA, ni, lre, ALU.mult, ["Et", "lre"], ["tA"])
    tt("dve", tB, nr, Aim, ALU.mult, ["nr", "Aim"], ["tB"])
    tt("dve", tA, tA, tB, ALU.subtract, ["tA", "tB"], ["tA"])
    tt("dve", rim, tA, den, ALU.mult, ["tA", "den"], ["rim"])
    Bbr = t32a(512); Bbi = t32a(512); u1 = t32a(512); u2 = t32a(512); u3 = t32a(512); u4 = t32a(512)

    def v3(a):
        return a.rearrange("p (q h) -> p q h", q=32)

    def bq(a):
        return a.unsqueeze(2).broadcast_to([128, 32, 16])

    tt("dve", v3(u1), v3(Bre), bq(rr), ALU.mult, ["Bre", "rr"], ["u1"])
    tt("dve", v3(u2), v3(Bim), bq(rim), ALU.mult, ["Bim", "rim"], ["u2"])
    tt("dve", Bbr, u1, u2, ALU.subtract, ["u1", "u2"], ["Bbr"])
    tt("dve", v3(u1), v3(Bim), bq(rr), ALU.mult, ["Bim", "rr"], ["u1"])
    tt("dve", v3(u2), v3(Bre), bq(rim), ALU.mult, ["Bre", "rim"], ["u2"])
    tt("dve", Bbi, u1, u2, ALU.add, ["u1", "u2"], ["Bbi"])
    BBD = hT[:, 0:2, :].rearrange("p a (b c) -> p (a b) c", c=32).rearrange("p (q r) c -> p q r c", r=2)
    ZB0 = hT[:, 2:4, :].rearrange("p a (b c) -> p (a b) c", c=32).rearrange("p (q r) c -> p q r c", r=2)
    memset("pool", hT[:, 0:4, :], 0.0, ["BBD", "ZB0"])
    memset("pool", WOUT[:], 0.0, ["WOUT"])
    P.add("act", lambda e: e.memzero(A[:]), (), ["XBDlo"])
    P.add("act", lambda e: e.memzero(Bz[:]), (), ["XBDhi"])
    memset("pool", TK[:], 0.0, ["TK"])
    for gp in range(2):
        ps_ = slice(64 * gp, 64 * gp + 64)
        cs_ = slice(16 * gp, 16 * gp + 16)
        cp("dve", BBD[ps_, :, 0, cs_], v3(Bbr)[ps_], ["Bbr"], ["BBD"])
        cp("dve", BBD[ps_, :, 1, cs_], v3(Bbi)[ps_], ["Bbi"], ["BBD"])
        cp("dve", ZB0[ps_, :, 0, cs_], v3(Cre)[ps_], ["Cre"], ["ZB0"])
        ts("dve", ZB0[ps_, :, 1, cs_], v3(Cim)[ps_], -1.0, None, ALU.mult, None, ["Cim"], ["ZB0"])
    def XBD(q0, q1):
        buf = A if q0 < 16 else Bz
        v = buf[:].rearrange("p a (b c) -> p (a b) c", c=32).rearrange("p (q s r) c -> p q s r c", s=8, r=2)
        return v, (0 if q0 < 16 else 16)

    for e_ in range(9):
        Er = bq(Etv[:, e_, 0, :]); Ei = bq(Etv[:, e_, 1, :])
        if e_ >= 1:
            tau = e_ - 1
            tt("dve", v3(u1), v3(Cre), Er, ALU.mult, ["Cre", "Et"], ["u1"])
            tt("dve", v3(u2), v3(Cim), Ei, ALU.mult, ["Cim", "Et"], ["u2"])
            tt("dve", v3(u3), v3(Cre), Ei, ALU.mult, ["Cre", "Et"], ["u3"])
            tt("dve", v3(u4), v3(Cim), Er, ALU.mult, ["Cim", "Et"], ["u4"])
            for gp in range(2):
                ps_ = slice(64 * gp, 64 * gp + 64)
                cs_ = slice(16 * gp, 16 * gp + 16)
                tt("dve", WOUT[ps_, :, tau, 0, cs_], v3(u1)[ps_], v3(u2)[ps_], ALU.subtract, ["u1", "u2"], ["WOUT"])
                stt("dve", WOUT[ps_, :, tau, 1, cs_], v3(u3)[ps_], -1.0, v3(u4)[ps_], ALU.mult, ALU.subtract,
                    ["u3", "u4"], ["WOUT"])
        if e_ <= 7:
            sg_ = 7 - e_
            if e_ <= 0:
                en_, w1, w2, w3, w4 = "dve", u1, u2, u3, u4
                n1, n2, n3, n4 = "u1", "u2", "u3", "u4"
            else:
                BTf = BiasT[:].rearrange("p h t -> p (h t)")
                en_, w1, w2, w3, w4 = "pool", gBf32[:, 0:512], gBf32[:, 512:1024], BTf[:, 0:512], BTf[:, 512:1024]
                n1, n2, n3, n4 = "u5", "u6", "BiasT", "BiasT"
            tt(en_, v3(w1), v3(Bbr), Er, ALU.mult, ["Bbr", "Et"], [n1])
            tt(en_, v3(w2), v3(Bbi), Ei, ALU.mult, ["Bbi", "Et"], [n2])
            tt(en_, v3(w3), v3(Bbi), Er, ALU.mult, ["Bbi", "Et"], [n3])
            tt(en_, v3(w4), v3(Bbr), Ei, ALU.mult, ["Bbr", "Et"], [n4])
            for half in range(2):
                xv_, q0 = XBD(16 * half, 16 * half + 16)
                nm = "XBDlo" if half == 0 else "XBDhi"
                qs_ = slice(16 * half, 16 * half + 16)
                for gp in range(2):
                    ps_ = slice(64 * gp, 64 * gp + 64)
                    cs_ = slice(16 * gp, 16 * gp + 16)
                    tt(en_, xv_[ps_, :, sg_, 0, cs_], v3(w1)[ps_, qs_], v3(w2)[ps_, qs_], ALU.subtract, [n1, n2, nm], [nm + "_%d_%d" % (sg_, gp)])
                    tt(en_, xv_[ps_, :, sg_, 1, cs_], v3(w3)[ps_, qs_], v3(w4)[ps_, qs_], ALU.add, [n3, n4, nm], [nm + "_%d_%d" % (sg_, gp)])
    Wmb = xs[:, 0, :].rearrange("p (h s) -> p h s", h=8)
    tt("dve", Wmb, Ws32.rearrange("p (h s) -> p h s", h=8), tril32.unsqueeze(1).broadcast_to([128, 8, 128]),
       ALU.mult, ["Ws32", "tril"], ["xs0"])
    for h in range(8):
        tr(PT[:, h // 4, (h % 4) * 128:(h % 4 + 1) * 128], Wmb[:, h, :], ["xs0"], ["PT%d" % (h // 4)])
    for kh in range(2):
        cp("dve", WmT[:, 4 * kh:4 * kh + 4, :], PT[:, kh, 0:512].rearrange("p (h t) -> p h t", h=4), ["PT%d" % kh], ["WmT"])
    for kh in range(2):
        mm(PS[:, kh, :], onesb[:], WmT[:, 4 * kh:4 * kh + 4, :], True, True, ["ones", "WmT"], ["ps%d" % kh])
        for hh in range(4):
            h = 4 * kh + hh
            stt("dve", BiasT[:, h, :], PS[:, kh, hh * 128:(hh + 1) * 128], lnb[:, h:h + 1],
                bsB32.rearrange("p (h t) -> p h t", h=8)[:, h, :], ALU.mult, ALU.add,
                ["ps%d" % kh, "lnb", "bsB"], ["BiasT"])

    for j in range(8):
        bk = 2 + (j % 2)
        for i in range(4):
            q = 4 * j + i
            osl = PS[32 * i:32 * i + 32, bk, :]
            mm(osl[:, 0:32], BBD[:, q, 0, :], ZB0[:, q, 0, :], True, False, ["BBD", "ZB0"], ["ps%d" % bk], tp=(0, 32 * i), sgc=True)
            mm(osl[:, 0:32], BBD[:, q, 1, :], ZB0[:, q, 1, :], False, True, ["BBD", "ZB0"], ["ps%d" % bk], tp=(0, 32 * i), sgc=True)
            o2 = osl[:, 32:256].rearrange("p (k c) -> p k c", k=7)
            mm(o2, BBD[:, q, 0, :], WOUT[:, q, 0:7, 0, :], True, False, ["BBD", "WOUT"], ["ps%d" % bk], tp=(0, 32 * i), sgc=True)
            mm(o2, BBD[:, q, 1, :], WOUT[:, q, 0:7, 1, :], False, True, ["BBD", "WOUT"], ["ps%d" % bk], tp=(0, 32 * i), sgc=True)
        for i in range(4):
            sl_ = slice(32 * i, 32 * i + 32)
            cp("act" if i % 2 else "dve", TK[sl_, j, 1:8, sl_],
               PS[sl_, bk, 32:256].rearrange("p (k c) -> p k c", k=7), ["ps%d" % bk], ["TK"])
            stt("dve", TK[sl_, j, 0, sl_], idt32[sl_, sl_], Dl[sl_, j:j + 1], PS[sl_, bk, 0:32], ALU.mult, ALU.add,
                ["ps%d" % bk, "pp_id", "Dl"], ["TK"])
    cnt = 0
    for j in range(8):
        for g4 in range(4):
            bk = 2 + (cnt % 4)
            cnt += 1
            for s4 in range(4):
                sr = 4 * g4 + s4
                sg_, ri = sr // 2, sr % 2
                for i in range(4):
                    q = 4 * j + i
                    xv_, q0 = XBD(q, q + 1)
                    nm = "XBDlo" if q < 16 else "XBDhi"
                    mm(PS[32 * i:32 * i + 32, bk, s4 * 128:(s4 + 1) * 128], xv_[:, q - q0, sg_, ri, :], identb[:], True, True,
                       [nm, nm + "_%d_0" % sg_, nm + "_%d_1" % sg_, "ident"], ["ps%d" % bk], tp=(0, 32 * i))
            cp("act" if cnt % 2 else "dve", WIN[:, j, 4 * g4:4 * g4 + 4, :],
               PS[:, bk, :].rearrange("p (s c) -> p s c", s=4), ["ps%d" % bk], ["WIN"])
    memset("dve", hst[:], 0.0, ["hst"])
    dma("sp", gBf32[:], gBf, (), ["gBf"], "pp3", waitfor=["u5", "u6"])
    PREP_BUFS_A = []
    PREP_BUFS_B = []
    PREP_BUFS_H = []
    PREP_XV = []
    P.barrier()

    ring_use = [0]

    WBASE = {"w_in0": 0, "w_glu": 4, "w_out0": 6, "w_in1": 8, "w_out1": 14}
    cur_step = [0]

    preloaded = {}

    def preload_slab(W, s, step):
        old = cur_step[0]
        cur_step[0] = step
        preloaded[(W.tensor.name, s)] = load_slab(W, s)
        cur_step[0] = old

    def load_slab(W, s):
        if (W.tensor.name, s) in preloaded:
            return preloaded.pop((W.tensor.name, s))
        slot = ring_use[0] % 2
        ring_use[0] += 1
        wid = WBASE[W.tensor.name] + s
        if cur_step[0] == 0:
            dma("pool", ring[:, slot, :, :], W[:, 512 * s:512 * s + 512].rearrange("(k p) c -> p k c", p=128),
                (), ["ring%d" % slot], "ring%d" % slot)
            dma("sp", wscr[wid], ring[:, slot, :, :], ["ring%d" % slot], ["wscr%d" % wid], "wscr%d" % wid)
        else:
            dma("sp", ring[:, slot, :, :], wscr[wid], ["wscr%d" % wid], ["ring%d" % slot], "ring%d" % slot)
        return slot

    Astg = A[:, 0:4, :].rearrange("p a (b c) -> p (a b) c", c=512)
    ASTG_A = ["A%d_%d" % (j_, b_) for j_ in range(4) for b_ in range(2)]

    Astg0 = A[:, 4:8, :].rearrange("p a (b c) -> p (a b) c", c=512)
    ASTG_A0 = ["A%d_%d" % (j_, b_) for j_ in range(4, 8) for b_ in range(2)]

    def load_slab_stage(W, s, hi=False):
        wid = WBASE[W.tensor.name] + s
        dst, nm, key, wf = (Astg0, "Astage0", "astage0", ASTG_A0) if hi else (Astg, "Astage", "astage", ASTG_A)
        if cur_step[0] == 0:
            dma("pool", dst, W[:, 512 * s:512 * s + 512].rearrange("(k p) c -> p k c", p=128),
                (), [nm], key, waitfor=wf)
            dma("sp", wscr[wid], dst, [nm], ["wscr%d" % wid], "wscr%d" % wid)
        else:
            dma("sp", dst, wscr[wid], ["wscr%d" % wid], [nm], key, waitfor=wf)

    dbank = [0]

    def next_dbank():
        b_ = dbank[0] % 2
        dbank[0] += 1
        return b_

    sbank = [0]

    def next_sbank():
        b_ = 2 + (sbank[0] % 4)
        sbank[0] += 1
        return b_

    def hT_res(b):
        return ["hT%d_%d" % (m, kh) for m in range(4 * b, 4 * b + 4) for kh in range(2)]

    def rms_s1(m, gBb, so, tag):
        xm = xsub(m)
        xb = m % 2
        act(xs[:, xb, :], xm, AF.Square, [XVR[m]], ["xs%d" % xb], accum=stat[:, so + m:so + m + 1])
        act(stat[:, so + 8 + m:so + 9 + m], stat[:, so + m:so + m + 1], AF.Sqrt, ["xs%d" % xb], [tag + "a%d" % m],
            bias=epsc[:, 0:1], scale=1.0 / D)
        recip(stat[:, so + 16 + m:so + 17 + m], stat[:, so + 8 + m:so + 9 + m], [tag + "a%d" % m], [tag + "b%d" % m])
        stt("dve", xs[:, xb, :], xm, stat[:, so + 16 + m:so + 17 + m], gBb[:], ALU.mult, ALU.mult,
            [XVR[m], tag + "b%d" % m, "gB0", "gB1"], ["xs%d" % xb])

    def rms_s2(m):
        xb = m % 2
        for kh in range(2):
            for kk in range(4):
                k = 4 * kh + kk
                tr(PT[:, kh, kk * 128:(kk + 1) * 128], xs[:, xb, k * 128:(k + 1) * 128], ["xs%d" % xb], ["PT%d" % kh])
            cp("act" if kh == 0 else "dve", hT[:, 4 * kh:4 * kh + 4, m * 128:(m + 1) * 128],
               PT[:, kh, 0:512].rearrange("p (k t) -> p k t", k=4), ["PT%d" % kh],
               ["hT%d_%d" % (m, kh)], waitfor=["Hh0a", "Hh0b", "Hh1a", "Hh1b"])

    def rms_order():
        o = [("s1", 0), ("s1", 1)]
        for m in range(8):
            o.append(("s2", m))
            if m + 2 < 8:
                o.append(("s1", m + 2))
        return o

    def run_rms(order, ptr, upto_m, gBb, so, tag):
        while ptr < len(order) and order[ptr][1] <= upto_m:
            kind, m = order[ptr]
            if kind == "s1":
                rms_s1(m, gBb, so, tag)
            else:
                rms_s2(m)
            ptr += 1
        return ptr

    def load_x_m(n, m):
        b, blk = m // 4, m % 4
        dma("sp", xsub(m), x2[b, TS * n + 128 * blk:TS * n + 128 * (blk + 1), :], (), [XVR[m]], "xm%d" % m,
            waitfor=["Vh0", "Vh1"])

    def reload_x(n):
        for b in range(2):
            dma("sp", XV[:, 4096 * b:4096 * (b + 1)].rearrange("p (m d) -> p m d", m=4),
                x2[b, TS * n:TS * (n + 1), :].rearrange("(m p) d -> p m d", p=128),
                (), [XVR[m] for m in range(4 * b, 4 * b + 4)], "x%d" % b, waitfor=["Vh0", "Vh1"])

    def dense_fm(W, s_list, evac, src=None, b_outer=False, hooks=None):
        if b_outer:
            assert len(s_list) == 2 and src is None
            slots = [load_slab(W, s) for s in s_list]
            tix = 0
            for b in range(2):
                for si, s in enumerate(s_list):
                    for jj in range(4):
                        if hooks and tix in hooks:
                            hooks[tix]()
                        tix += 1
                        bk = next_dbank()
                        for k in range(8):
                            mm(PS[:, bk, :], ring[:, slots[si], k, jj * 128:(jj + 1) * 128], hT[:, k, b * 512:(b + 1) * 512],
                               k == 0, k == 7, ["ring%d" % slots[si]] + hT_res(b), ["ps%d" % bk])
                        evac(s, jj, b, bk)
            return
        for s in s_list:
            slot = load_slab(W, s)
            for jj in range(4):
                for b in range(2):
                    bk = next_dbank()
                    for k in range(8):
                        if src is None:
                            rhs, rres = hT[:, k, b * 512:(b + 1) * 512], hT_res(b)
                        else:
                            rhs, rres = A[:, k, b * 512:(b + 1) * 512], ["A%d_%d" % (k, b)]
                        mm(PS[:, bk, :], ring[:, slot, k, jj * 128:(jj + 1) * 128], rhs,
                           k == 0, k == 7, ["ring%d" % slot] + rres, ["ps%d" % bk])
                    evac(s, jj, b, bk)

    def tm_tiles(W, post_m, pre=None, stage_both=False, skip_loads=False):
        if stage_both:
            if not skip_loads:
                load_slab_stage(W, 0, hi=True)
            slot0 = None
        else:
            slot0 = load_slab(W, 0)
        if not skip_loads:
            load_slab_stage(W, 1)
        if pre is not None:
            pre()
        yield
        for m in range(8):
            b = m // 4
            for dh in range(2):
                bk = next_dbank()
                for k in range(8):
                    if dh == 0 and slot0 is None:
                        rhs, rres = Astg0[:, k, :], "Astage0"
                    elif dh == 0:
                        rhs, rres = ring[:, slot0, k, :], "ring%d" % slot0
                    else:
                        rhs, rres = Astg[:, k, :], "Astage"
                    mm(PS[:, bk, :], Bz[:, k, m * 128:(m + 1) * 128], rhs, k == 0, k == 7,
                       [rres, "B%d_%d" % (k, b)], ["ps%d" % bk])
                tt("dve", xsub(m)[:, dh * 512:(dh + 1) * 512], PS[:, bk, :], xsub(m)[:, dh * 512:(dh + 1) * 512], ALU.add,
                   ["ps%d" % bk, XVR[m]], [XVR[m]])
                if dh == 1 and m >= 1:
                    post_m(m - 1)
                yield
        post_m(7)

    def dense_tm_out(W, post_m):
        slots = [load_slab(W, 0), load_slab(W, 1)]
        for m in range(8):
            b = m // 4
            for dh in range(2):
                bk = next_dbank()
                for k in range(8):
                    mm(PS[:, bk, :], Bz[:, k, m * 128:(m + 1) * 128], ring[:, slots[dh], k, :], k == 0, k == 7,
                       ["ring%d" % slots[dh], "B%d_%d" % (k, b)], ["ps%d" % bk])
                tt("dve", xsub(m)[:, dh * 512:(dh + 1) * 512], PS[:, bk, :], xsub(m)[:, dh * 512:(dh + 1) * 512], ALU.add,
                   ["ps%d" % bk, XVR[m]], [XVR[m]])
            if m >= 1:
                post_m(m - 1)
        post_m(7)

    AR2r = ARI[:, 0, 0, :].unsqueeze(2).broadcast_to([128, 32, 2])
    ARn = ARI[:, 1, 0, :].unsqueeze(2).broadcast_to([128, 32, 2])
    ARp = ARI[:, 1, 1, :].unsqueeze(2).broadcast_to([128, 32, 2])

    def h4(a):
        return a.rearrange("p r (q b) -> p r q b", b=2)

    def h3(a):
        return a.rearrange("p (q b) -> p q b", b=2)

    Hall = hTf.rearrange("p (r q b c) -> p r q b c", r=2, q=32, b=2)
    Vall = XV[:, :].rearrange("p (r q b c) -> p r q b c", r=2, q=32, b=2)

    def ttn(out, in0, in1, op, reads, writes, waitfor=()):
        P.add("dve", lambda e, out=out, in0=in0, in1=in1, op=op: e.tensor_tensor(out=out, in0=in0, in1=in1, op=op),
              reads, writes, waitfor=waitfor, noself=True)

    L0_ORDER = rms_order()
    l0_ptr = 0
    for m in range(8):
        load_x_m(0, m)

    for n in range(NSTEP):
        cur_step[0] = n
        hooks = None
        if n > 0:
            pend = L0_ORDER[l0_ptr:]
            sched = {}
            slot_ = 2
            st = {"p": l0_ptr}

            def mk(upto_idx):
                def f():
                    while st["p"] <= upto_idx:
                        kind, m = L0_ORDER[st["p"]]
                        if kind == "s1":
                            rms_s1(m, gB0b, 0, "r0")
                        else:
                            rms_s2(m)
                        st["p"] += 1
                return f
            idxs = list(range(l0_ptr, len(L0_ORDER)))
            lead = l0_ptr
            while lead < len(L0_ORDER) and L0_ORDER[lead][0] == "s1":
                lead += 1
            mk(lead - 1)()
            rest = list(range(st["p"], len(L0_ORDER)))
            hooks = {}
            for r_i, li in enumerate(rest):
                hooks[min(7, 2 + 3 * r_i)] = mk(li)
            hooks[8] = mk(len(L0_ORDER) - 1)
            l0_ptr = len(L0_ORDER)
        else:
            l0_ptr = run_rms(L0_ORDER, l0_ptr, 7, gB0b, 0, "r0")

        def ev_in0(s, jj, b, bk):
            if s < 2:
                j = 4 * s + jj
                cp("dve", A[:, j, b * 512:(b + 1) * 512], PS[:, bk, :], ["ps%d" % bk], ["A%d_%d" % (j, b)], waitfor=["Astage", "Astage0"])
            else:
                j = 4 * (s - 2) + jj
                act(Bz[:, j, b * 512:(b + 1) * 512], PS[:, bk, :], AF.Silu, ["ps%d" % bk], ["B%d_%d" % (j, b)])
        dense_fm(w_in0, [0, 1], ev_in0, b_outer=True, hooks=hooks)

        for j in range(8):
            for ri in range(2):
                col = ((j % 2) * 2 + ri) * 128
                for sg_ in range(8):
                    for i in range(4):
                        mm(PS[:, 2 + i, col:col + 128], WIN[32 * i:32 * i + 32, j, 2 * sg_ + ri, :],
                           A[32 * i:32 * i + 32, j, :].rearrange("p (c t) -> p c t", t=8)[:, :, sg_],
                           sg_ == 0, sg_ == 7, ["WIN", "A%d_0" % j, "A%d_1" % j], ["ps%d" % (2 + i)], tp=(32 * i, 0))
            for i in range(4):
                q = 4 * j + i
                for ri in range(2):
                    col = ((j % 2) * 2 + ri) * 128
                    cp("act" if i % 2 else "dve", Vv(ri)[:, q, :, :],
                       PS[:, 2 + i, col:col + 128].rearrange("p (b c) -> p b c", b=2),
                       ["ps%d" % (2 + i)], [XVR[4 * ri + q // 8]])
        for ct in range(64):
            if ct == 0:
                hr_, hi_ = h3(hst[:, 0, :]), h3(hst[:, 1, :])
                hres = ["hst"]
            else:
                hr_, hi_ = Vall[:, 0, :, :, ct - 1], Vall[:, 1, :, :, ct - 1]
                hres = ["Vh%d" % ((ct - 1) // 32)]
            vres = "Vh%d" % (ct // 32)
            wf = XVR if ct == 0 else ()
            ttn(h3(rt[:, 0, 0, :]), hr_, AR2r, ALU.mult, hres + ["ARI"], ["rtA"], waitfor=wf)
            ttn(h3(rt[:, 0, 1, :]), hi_, AR2r, ALU.mult, hres + ["ARI"], ["rtB"])
            ttn(h3(rt[:, 1, 0, :]), hi_, ARn, ALU.mult, hres + ["ARI"], ["rtC"])
            ttn(h3(rt[:, 1, 1, :]), hr_, ARp, ALU.mult, hres + ["ARI"], ["rtD"])
            ttn(rt[:, 2, 0, :], rt[:, 0, 0, :], rt[:, 1, 0, :], ALU.add, ["rtA", "rtC"], ["rtE"])
            ttn(rt[:, 2, 1, :], rt[:, 0, 1, :], rt[:, 1, 1, :], ALU.add, ["rtB", "rtD"], ["rtF"])
            ttn(Vall[:, 0, :, :, ct], h3(rt[:, 2, 0, :]), Vall[:, 0, :, :, ct], ALU.add, ["rtE", vres], [vres])
            ttn(Vall[:, 1, :, :, ct], h3(rt[:, 2, 1, :]), Vall[:, 1, :, :, ct], ALU.add, ["rtF", vres], [vres])
        dense_fm(w_in0, [2, 3], ev_in0)

        for h2 in range(2):
            if h2 == 0:
                cp("act", Hall[:, :, :, :, 0], h4(hst[:, 0:2, :]), ["hst"], HBR + ["Hh0a", "Hh0b"])
                cp("act", Hall[:, :, 0:16, :, 1:32], Vall[:, :, 0:16, :, 0:31], ["Vh0"], HBR + ["Hh0a"])
                cp("act", Hall[:, :, 16:32, :, 1:32], Vall[:, :, 16:32, :, 0:31], ["Vh0"], HBR + ["Hh0b"])
            else:
                cp("act", Hall[:, :, 0:16, :, 32:64], Vall[:, :, 0:16, :, 31:63], ["Vh0", "Vh1"], HBR + ["Hh1a"])
                cp("act", Hall[:, :, 16:32, :, 32:64], Vall[:, :, 16:32, :, 31:63], ["Vh0", "Vh1"], HBR + ["Hh1b"])
                cp("dve", h4(hst[:, 0:2, :]), Vall[:, :, :, :, 63], ["Vh1"], ["hst"])
            for jg in range(2):
                bks = {}
                for j in range(4 * jg, 4 * jg + 4):
                    bk = next_sbank()
                    bks[j] = bk
                    ures = ["A%d_0" % j, "A%d_1" % j]
                    uv = A[:, j, :].rearrange("p (b c t) -> p b c t", b=2, t=8)[:, :, 32 * h2:32 * h2 + 32, :]
                    ov = PS[:, bk, :].rearrange("p (b c t) -> p b c t", b=2, t=8)
                    for k in range(8):
                        mm(ov[:, :, :, k:8], TK[:, j, k, :], uv[:, :, :, 0:8 - k], k == 0, False, ["TK"] + ures, ["ps%d" % bk], sgc=True)
                for j in range(4 * jg, 4 * jg + 4):
                    bk = bks[j]
                    ures = ["A%d_0" % j, "A%d_1" % j]
                    for tau in range(8):
                        for ri in range(2):
                            for i in range(4):
                                q = 4 * j + i
                                last = (i == 3 and tau == 7 and ri == 1)
                                mm(PS[32 * i:32 * i + 32, bk, :].rearrange("p (b c t) -> p b c t", b=2, t=8)[:, :, :, tau],
                                   WOUT[:, q, tau, ri, :], Hv(ri)[:, q, :, 32 * h2:32 * h2 + 32], False, last,
                                   ["WOUT", "Hh%d%s" % (h2, "a" if q < 16 else "b")], ["ps%d" % bk], tp=(0, 32 * i), sgc=True)
                    av_ = A[:, j, :].rearrange("p (b x) -> p b x", b=2)[:, :, 256 * h2:256 * h2 + 256]
                    bv_ = Bz[:, j, :].rearrange("p (b x) -> p b x", b=2)[:, :, 256 * h2:256 * h2 + 256]
                    act(av_, PS[:, bk, :].rearrange("p (b x) -> p b x", b=2), AF.Gelu, ["ps%d" % bk], ures)
                    tt("dve", bv_, bv_, av_, ALU.mult, ures + ["B%d_0" % j, "B%d_1" % j], ["B%d_0" % j, "B%d_1" % j])
        def ev_glu(s, jj, b, bk):
            j = 4 * s + jj
            eb = (2 * j + b) % 2
            act(evb[:, eb, :], PS[:, bk, :], AF.Sigmoid, ["ps%d" % bk, "bglu"], ["evb%d" % eb], bias=bglu[:, j:j + 1])
            tt("dve", Bz[:, j, b * 512:(b + 1) * 512], evb[:, eb, :], Bz[:, j, b * 512:(b + 1) * 512], ALU.mult,
               ["evb%d" % eb, "B%d_%d" % (j, b)], ["B%d_%d" % (j, b)])
        dense_fm(w_glu, [0, 1], ev_glu, src="A")
        reload_x(n)
        L1_ORDER = rms_order()
        l1 = [0]

        def post0(m):
            l1[0] = run_rms(L1_ORDER, l1[0], m, gB1b, 0, "r1")
        for _ in tm_tiles(w_out0, post0, pre=lambda n=n: preload_slab(w_in1, 0, n)):
            pass
        l1[0] = run_rms(L1_ORDER, l1[0], 7, gB1b, 0, "r1")

        def ev_in1(s, jj, b, bk):
            if s < 2:
                j = 4 * s + jj
                act(A[:, j, b * 512:(b + 1) * 512], PS[:, bk, :], AF.Gelu, ["ps%d" % bk], ["A%d_%d" % (j, b)], waitfor=["Astage", "Astage0"])
            else:
                j = 4 * (s - 4) + jj
                act(Bz[:, j, b * 512:(b + 1) * 512], PS[:, bk, :], AF.Silu, ["ps%d" % bk], ["B%d_%d" % (j, b)])
                tt("dve", Bz[:, j, b * 512:(b + 1) * 512], Bz[:, j, b * 512:(b + 1) * 512], A[:, j, b * 512:(b + 1) * 512], ALU.mult,
                   ["B%d_%d" % (j, b), "A%d_%d" % (j, b)], ["B%d_%d" % (j, b)])
        dense_fm(w_in1, [0, 1, 4, 5], ev_in1)
        vslots = [load_slab(w_in1, 2), load_slab(w_in1, 3)]
        load_slab_stage(w_out1, 0, hi=True)
        load_slab_stage(w_out1, 1)
        vb = [0]
        ln_deferred = []

        def ln_stats(g_):
            so = 48 + 24 * g_
            ms_ = slice(4 * g_, 4 * g_ + 4)
            S12v = S12[:].rearrange("p m w h -> p w m h")
            tg = "ln%d" % g_
            tt("dve", stat[:, so:so + 8].rearrange("p (w m) -> p w m", w=2), S12v[:, :, ms_, 0], S12v[:, :, ms_, 1], ALU.add,
               ["evb0", "evb1"] + HBR, [tg + "A"])
            ts("dve", stat[:, so:so + 8], stat[:, so:so + 8], 1.0 / D, None, ALU.mult, None, [tg + "A"], [tg + "A"])
            tt("dve", stat[:, so + 8:so + 12], stat[:, so:so + 4], stat[:, so:so + 4], ALU.mult, [tg + "A"], [tg + "B"])
            tt("dve", stat[:, so + 12:so + 16], stat[:, so + 4:so + 8], stat[:, so + 8:so + 12], ALU.subtract,
               [tg + "A", tg + "B"], [tg + "C"])
            act(stat[:, so + 16:so + 20], stat[:, so + 12:so + 16], AF.Sqrt, [tg + "C"], [tg + "D"], bias=epsc[:, 1:2])
            recip(stat[:, so + 20:so + 24], stat[:, so + 16:so + 20], [tg + "D"], [tg + "E"])
            for mm_ in range(4 * g_, 4 * g_ + 4):
                blkv = hT[:, :, mm_ * 128:(mm_ + 1) * 128]
                ts("dve", blkv, blkv, stat[:, so + (mm_ % 4):so + (mm_ % 4) + 1], stat[:, so + 20 + (mm_ % 4):so + 21 + (mm_ % 4)],
                   ALU.subtract, ALU.mult,
                   ["hT%d_0" % mm_, "hT%d_1" % mm_, tg + "A", tg + "E"], ["hT%d_0" % mm_, "hT%d_1" % mm_])

        for m in range(8):
            bks = []
            for half in range(2):
                bk = [0, 1, 4, 5][vb[0] % 4]
                vb[0] += 1
                bks.append(bk)
                for k in range(8):
                    mm(PS[:, bk, :], hT[:, k, m * 128:(m + 1) * 128], ring[:, vslots[half], k, :], k == 0, k == 7,
                       ["ring%d" % vslots[half], "hT%d_%d" % (m, k // 4)], ["ps%d" % bk])
            for half in range(2):
                bk = bks[half]
                gdst = hT[:, 4 * half:4 * half + 4, m * 128:(m + 1) * 128]
                act(gdst, PS[:, bk, :].rearrange("p (h c) -> p h c", h=4), AF.Gelu, ["ps%d" % bk], ["hT%d_%d" % (m, half)],
                    accum=S12[:, m, 0, half:half + 1])
                stt("dve", evb[:, half, :].rearrange("p (h c) -> p h c", h=4), gdst, 1.0, gdst, ALU.mult, ALU.mult,
                    ["hT%d_%d" % (m, half)], ["evb%d" % half], accum=S12[:, m, 1, half:half + 1])
            if m == 3:
                ln_stats(0)
            if m == 7:
                ln_deferred.append(lambda: ln_stats(1))
        def mix_tile(h, b, idx):
            bk = next_sbank()
            for blk in range(4):
                m = 4 * b + blk
                mm(PS[:, bk, blk * 128:(blk + 1) * 128], hT[:, h, m * 128:(m + 1) * 128], WmT[:, h, :], True, True,
                   ["hT%d_%d" % (m, h // 4), "WmT"], ["ps%d" % bk])
            eb = idx % 2
            stt("dve", ev32[:, eb, :].rearrange("p (k t) -> p k t", k=4), PS[:, bk, :].rearrange("p (k t) -> p k t", k=4),
                lng[:, h:h + 1], BiasT[:, h, :].unsqueeze(1).broadcast_to([128, 4, 128]), ALU.mult, ALU.add,
                ["ps%d" % bk, "lng", "BiasT"], ["ev32_%d" % eb])
            tt("pool" if idx % 2 else "dve", Bz[:, h, b * 512:(b + 1) * 512], ev32[:, eb, :], Bz[:, h, b * 512:(b + 1) * 512], ALU.mult,
               ["ev32_%d" % eb, "B%d_%d" % (h, b)], ["B%d_%d" % (h, b)])
        l0_ptr = 0
        nxt = [0]

        def post1(m, n=n):
            xm = xsub(m)
            xb = m % 2
            fo = 24
            act(evb[:, xb, :].rearrange("p (a c) -> p a c", a=1)[:, 0, :] if False else ev32[:, xb, :], xm[:, 0:512], AF.Square,
                [XVR[m]], ["ev32_%d" % xb], accum=stat[:, fo + m:fo + m + 1])
            act(evb[:, xb, :], xm[:, 512:1024], AF.Square, [XVR[m]], ["evb%d" % xb], accum=stat[:, fo + 8 + m:fo + 9 + m])
            tt("dve", stat[:, fo + m:fo + m + 1], stat[:, fo + m:fo + m + 1], stat[:, fo + 8 + m:fo + 9 + m], ALU.add,
               ["ev32_%d" % xb, "evb%d" % xb], ["fa%d" % m])
            act(stat[:, fo + 8 + m:fo + 9 + m], stat[:, fo + m:fo + m + 1], AF.Sqrt, ["fa%d" % m], ["fb%d" % m],
                bias=epsc[:, 0:1], scale=1.0 / D)
            recip(stat[:, fo + 16 + m:fo + 17 + m], stat[:, fo + 8 + m:fo + 9 + m], ["fb%d" % m], ["fc%d" % m])
            tt("pool", xm, xm, gBf32[:], ALU.mult, [XVR[m], "gBf", "ev32_%d" % xb, "evb%d" % xb], [XVR[m]])
            act(xm, xm, AF.Copy, [XVR[m], "fc%d" % m], [XVR[m]], scale=stat[:, fo + 16 + m:fo + 17 + m])
            b, blk = m // 4, m % 4
            dma("sp", y2[b, TS * n + 128 * blk:TS * n + 128 * (blk + 1), :], xm, [XVR[m]], ["out%d" % m], "out%d" % m)
            if n + 1 < NSTEP:
                load_x_m(n + 1, m)
                nxt[0] = run_rms(L0_ORDER, nxt[0], m - 3, gB0b, 0, "r0")
        if n + 1 < NSTEP:
            preload_slab(w_in0, 0, n + 1)
            preload_slab(w_in0, 1, n + 1)
        g_out = tm_tiles(w_out1, post1, stage_both=True, skip_loads=True)
        next(g_out)
        for h in range(8):
            mix_tile(h, 0, h)
        for f_ in ln_deferred:
            f_()
        for h in range(8):
            next(g_out)
            mix_tile(h, 1, 8 + h)
        for _ in g_out:
            pass
        l0_ptr = nxt[0]
    P.add("sp", None, ["out%d" % m for m in range(8)], ())

    P.finalize()
    esem = {e: es.enter_context(nc.semaphore("sem_" + e)) for e in Prog.ENG}
    dsem = {k: es.enter_context(nc.semaphore("dsem_" + k)) for k in P.dmacnt}
    block = es.enter_context(nc.Block())

    @block.tensor
    def _(e):
        P.emit_engine("pe", e, esem, dsem)

    @block.scalar
    def _(e):
        P.emit_engine("act", e, esem, dsem)

    @block.vector
    def _(e):
        P.emit_engine("dve", e, esem, dsem)

    @block.gpsimd
    def _(e):
        P.emit_engine("pool", e, esem, dsem)

    @block.sync
    def _(e):
        P.emit_engine("sp", e, esem, dsem)

    es.close()
    return nc


def host_inputs(inp):
    f = lambda a: np.ascontiguousarray(np.asarray(a, dtype=np.float32))
    shared = {}
    shared["gB0"] = f(np.broadcast_to(inp["norm_g"][0][None, :], (128, D)))
    shared["gB1"] = f(np.broadcast_to(inp["norm_g"][1][None, :], (128, D)))
    shared["gBf"] = f(np.broadcast_to(inp["final_g"][None, :], (128, D)))
    shared["w_in0"] = f(inp["s5_w_in"][0])
    shared["w_glu"] = f(inp["s5_w_glu"][0])
    shared["w_out0"] = f(inp["s5_w_out"][0])
    shared["w_in1"] = f(inp["sgu_w_in"][0])
    shared["w_out1"] = f(inp["sgu_w_out"][0])

    def gp_layout(a):
        a = np.asarray(a)
        rest = a.shape[2:]
        a = a.reshape((32, 2, 64) + rest)
        a = np.moveaxis(a, 0, 2)
        return a.reshape((128, 32) + rest)
    shared["Are"] = f(gp_layout(inp["s5_A_re"][0]))
    shared["Aim"] = f(gp_layout(inp["s5_A_im"][0]))
    shared["Ldt"] = f(gp_layout(np.broadcast_to(np.asarray(inp["s5_log_dt"][0])[:, None], (64, 64))))
    shared["Bre"] = f(gp_layout(inp["s5_B_re"][0]).reshape(128, 512))
    shared["Bim"] = f(gp_layout(inp["s5_B_im"][0]).reshape(128, 512))
    shared["Cre"] = f(gp_layout(np.transpose(inp["s5_C_re"][0], (0, 2, 1))).reshape(128, 512))
    shared["Cim"] = f(gp_layout(np.transpose(inp["s5_C_im"][0], (0, 2, 1))).reshape(128, 512))
    shared["Dl"] = f(np.asarray(inp["s5_D"][0]).reshape(8, 128).T)
    shared["bglu"] = f(np.asarray(inp["s5_b_glu"][0]).reshape(8, 128).T)
    shared["Ws"] = f(np.transpose(inp["sgu_w_s"][0], (1, 0, 2)).reshape(128, 1024))
    shared["tril"] = f(np.tril(np.ones((128, 128), np.float32)))
    shared["lng"] = f(np.asarray(inp["sgu_ln_g"][0]).reshape(8, 128).T)
    shared["lnb"] = f(np.asarray(inp["sgu_ln_b"][0]).reshape(8, 128).T)
    shared["bsB"] = f(np.broadcast_to(np.asarray(inp["sgu_b_s"][0]).reshape(1, 1024), (128, 1024)))
    shared["ident"] = f(np.eye(128, dtype=np.float32))
    return shared


_NC_CACHE = {}


def kernel(**inputs):
    x = np.asarray(inputs["x"], dtype=np.float32)
    shared = host_inputs(inputs)
    if "nc" not in _NC_CACHE:
        _NC_CACHE["nc"] = build_nc()
    nc = _NC_CACHE["nc"]
    in_maps = []
    for c in range(NCORES):
        m = dict(shared)
        m["x2"] = np.ascontiguousarray(x[2 * c:2 * c + 2])
        in_maps.append(m)
    res = run_bass_kernel_spmd(nc, in_maps, core_ids=list(range(NCORES)))
    out = np.concatenate([np.asarray(r["y2"]) for r in res.results], axis=0)
    return out.astype(np.float32)
```

```python
import math
from contextlib import ExitStack
import numpy as np
import concourse.bass as bass
import concourse.mybir as mybir
from concourse.bass_utils import run_bass_kernel_spmd

F32 = mybir.dt.float32
BF16 = mybir.dt.bfloat16
AF = mybir.ActivationFunctionType
ALU = mybir.AluOpType

NCORES = 8
SEQ = 2048
D = 1024
NSTEP = 4
TS = 512
PI = math.pi


class Prog:
    ENG = ["pe", "act", "dve", "pool", "sp"]

    def __init__(self):
        self.ops = {e: [] for e in self.ENG}
        self.lastw = {}
        self.rd = {}
        self.dmacnt = {}

    def add(self, eng, fn, reads=(), writes=(), dma=None, waitfor=(), noself=False):
        deps = []
        for r in waitfor:
            w = self.lastw.get(r)
            if w is not None:
                deps.append(w)
            for e in self.rd.get(r, ()):
                if e[0] == "E" and e[1] == eng:
                    continue
                deps.append(e)
        for r in reads:
            w = self.lastw.get(r)
            if w is not None:
                deps.append(w)
        for r in writes:
            w = self.lastw.get(r)
            if w is not None:
                deps.append(w)
            for e in self.rd.get(r, ()):
                if e[0] == "E" and e[1] == eng:
                    continue
                deps.append(e)
        idx = len(self.ops[eng])
        if dma is not None:
            c = self.dmacnt.get(dma, 0) + 1
            self.dmacnt[dma] = c
            ev = ("D", dma, c)
        else:
            ev = ("E", eng, idx)
        keep = []
        seen = set()
        for d in deps:
            if d in seen:
                continue
            seen.add(d)
            if d[0] == "E" and d[1] == eng and (eng in ("pe", "sp") or noself):
                continue
            keep.append(d)
        self.ops[eng].append({"fn": fn, "deps": keep, "ev": ev, "signal": False})
        for r in reads:
            lst = self.rd.setdefault(r, [])
            if ev[0] == "E":
                lst[:] = [x for x in lst if not (x[0] == "E" and x[1] == eng)]
            else:
                lst[:] = [x for x in lst if not (x[0] == "D" and x[1] == ev[1])]
            lst.append(ev)
        for r in writes:
            self.lastw[r] = ev
            self.rd[r] = []
        return ev

    def barrier(self):
        lasts = []
        for e in self.ENG:
            for i in range(len(self.ops[e]) - 1, -1, -1):
                o = self.ops[e][i]
                if o["fn"] is not None and o["ev"][0] == "E":
                    lasts.append(o["ev"])
                    break
        dm = [("D", k, c) for k, c in self.dmacnt.items()]
        for e in self.ENG:
            deps = [d for d in lasts if d[1] != e] + dm
            self.ops[e].append({"fn": None, "deps": deps, "ev": ("E", e, len(self.ops[e])), "signal": False})
        self.lastw = {}
        self.rd = {}

    def finalize(self):
        for e in self.ENG:
            for o in self.ops[e]:
                for d in o["deps"]:
                    if d[0] == "E":
                        self.ops[d[1]][d[2]]["signal"] = True
        self.sigval = {}
        for e in self.ENG:
            c = 0
            for i, o in enumerate(self.ops[e]):
                if o["signal"]:
                    c += 1
                    self.sigval[(e, i)] = c

    def emit_engine(self, e, eng, esem, dsem):
        waited = {}
        for o in self.ops[e]:
            need = {}
            for d in o["deps"]:
                if d[0] == "E":
                    key = ("E", d[1])
                    val = self.sigval[(d[1], d[2])]
                else:
                    key = ("D", d[1])
                    val = 16 * d[2]
                if val > need.get(key, 0):
                    need[key] = val
            for key, val in need.items():
                if waited.get(key, 0) < val:
                    sem = esem[key[1]] if key[0] == "E" else dsem[key[1]]
                    eng.wait_ge(sem, val)
                    waited[key] = val
            if o["fn"] is None:
                continue
            ins = o["fn"](eng)
            if o["ev"][0] == "D":
                ins.then_inc(dsem[o["ev"][1]], 16)
            elif o["signal"]:
                ins.then_inc(esem[e], 1)


def build_nc(debug=False):
    nc = bass.Bass("TRN2", target_bir_lowering=False)
    P = Prog()

    def din(name, shape):
        return nc.dram_tensor(name, list(shape), F32, kind="ExternalInput").ap()

    x2 = din("x2", [2, SEQ, D])
    gB0 = din("gB0", [128, D])
    gB1 = din("gB1", [128, D])
    gBf = din("gBf", [128, D])
    w_in0 = din("w_in0", [D, 2 * D])
    w_glu = din("w_glu", [D, D])
    w_out0 = din("w_out0", [D, D])
    w_in1 = din("w_in1", [D, 3 * D])
    w_out1 = din("w_out1", [D, D])
    dAre = din("Are", [128, 32])
    dAim = din("Aim", [128, 32])
    dLdt = din("Ldt", [128, 32])
    dBre = din("Bre", [128, 512])
    dBim = din("Bim", [128, 512])
    dCre = din("Cre", [128, 512])
    dCim = din("Cim", [128, 512])
    dDl = din("Dl", [128, 8])
    dbglu = din("bglu", [128, 8])
    dWs = din("Ws", [128, 1024])
    dtril = din("tril", [128, 128])
    dlng = din("lng", [128, 8])
    dlnb = din("lnb", [128, 8])
    dbsB = din("bsB", [128, 1024])
    dident = din("ident", [128, 128])
    y2 = nc.dram_tensor("y2", [2, SEQ, D], F32, kind="ExternalOutput").ap()
    wscr = nc.dram_tensor("wscr", [16, 128, 8, 512], BF16, kind="Internal").ap()

    es = ExitStack()

    def sb(name, shape, dt):
        return es.enter_context(nc.sbuf_tensor("sb_" + name, list(shape), dt))

    XV = sb("XV", [128, 8192], F32)
    hT = sb("hT", [128, 8, 1024], BF16)
    A = sb("A", [128, 8, 1024], BF16)
    Bz = sb("Bz", [128, 8, 1024], BF16)
    TK = sb("TK", [128, 8, 8, 128], BF16)
    WIN = sb("WIN", [128, 8, 16, 128], BF16)
    WOUT = sb("WOUT", [128, 32, 8, 2, 32], BF16)
    ARI = sb("ARI", [128, 2, 2, 32], F32)
    ring = sb("ring", [128, 2, 8, 512], BF16)
    identb = sb("identb", [128, 128], BF16)
    onesb = sb("onesb", [128, 128], BF16)
    gB0b = sb("gB0b", [128, D], BF16)
    gB1b = sb("gB1b", [128, D], BF16)
    gBf32 = sb("gBf32", [128, D], F32)
    Dl = sb("Dl", [128, 8], F32)
    bglu = sb("bglu", [128, 8], F32)
    lng = sb("lng", [128, 8], F32)
    lnb = sb("lnb", [128, 8], F32)
    BiasT = sb("BiasT", [128, 8, 128], F32)
    WmT = sb("WmT", [128, 8, 128], BF16)
    xs = sb("xs", [128, 2, 1024], BF16)
    stat = sb("stat", [128, 112], F32)
    S12 = sb("S12", [128, 8, 2, 2], F32)
    hst = sb("hst", [128, 3, 64], F32)
    rt = sb("rt", [128, 3, 2, 64], F32)
    ev32 = sb("ev32", [128, 2, 512], F32)
    evb = sb("evb", [128, 2, 512], BF16)
    epsc = sb("epsc", [128, 2], F32)
    misc32 = sb("misc32", [128, 704], F32)
    PS = es.enter_context(nc.psum_tensor("PS", [128, 6, 512], F32))
    PT = es.enter_context(nc.psum_tensor("PT", [128, 2, 1024], BF16))

    def dma(q, out, in_, reads, writes, key, waitfor=()):
        P.add(q, lambda e, out=out, in_=in_: e.dma_start(out=out, in_=in_), reads, writes, dma=key, waitfor=waitfor)

    def mm(out, lhsT, rhs, start, stop, reads, writes, tp=None, sgc=False):
        def fn(e, out=out, lhsT=lhsT, rhs=rhs, start=start, stop=stop, tp=tp, sgc=sgc):
            kw = {}
            if tp is not None:
                kw["tile_position"] = tp
            if sgc:
                kw["skip_group_check"] = True
            return e.matmul(out, lhsT=lhsT, rhs=rhs, start=start, stop=stop, **kw)
        P.add("pe", fn, reads, writes)

    def tr(out, in_, reads, writes):
        P.add("pe", lambda e, out=out, in_=in_: e.transpose(out, in_, identb[:]), list(reads) + ["ident"], writes)

    def act(out, in_, func, reads, writes, bias=None, scale=None, accum=None, waitfor=()):
        def fn(e, out=out, in_=in_, func=func, bias=bias, scale=scale, accum=accum):
            kw = {}
            if bias is not None:
                kw["bias"] = bias
            if scale is not None:
                kw["scale"] = scale
            if accum is not None:
                kw["accum_out"] = accum
            return e.activation(out=out, in_=in_, func=func, **kw)
        P.add("act", fn, reads, writes, waitfor=waitfor)

    def cp(eng, out, in_, reads, writes, waitfor=()):
        if eng == "act":
            act(out, in_, AF.Copy, reads, writes, waitfor=waitfor)
        else:
            P.add(eng, lambda e, out=out, in_=in_: e.tensor_copy(out=out, in_=in_), reads, writes, waitfor=waitfor)

    def tt(eng, out, in0, in1, op, reads, writes, waitfor=()):
        P.add(eng, lambda e, out=out, in0=in0, in1=in1, op=op: e.tensor_tensor(out=out, in0=in0, in1=in1, op=op), reads, writes,
              waitfor=waitfor)

    def ts(eng, out, in0, s1, s2, op0, op1, reads, writes):
        def fn(e, out=out, in0=in0, s1=s1, s2=s2, op0=op0, op1=op1):
            if op1 is None:
                return e.tensor_scalar(out=out, in0=in0, scalar1=s1, scalar2=None, op0=op0)
            return e.tensor_scalar(out=out, in0=in0, scalar1=s1, scalar2=s2, op0=op0, op1=op1)
        P.add(eng, fn, reads, writes)

    def stt(eng, out, in0, scalar, in1, op0, op1, reads, writes, accum=None):
        def fn(e, out=out, in0=in0, scalar=scalar, in1=in1, op0=op0, op1=op1, accum=accum):
            if accum is None:
                return e.scalar_tensor_tensor(out=out, in0=in0, scalar=scalar, in1=in1, op0=op0, op1=op1)
            return e.scalar_tensor_tensor(out=out, in0=in0, scalar=scalar, in1=in1, op0=op0, op1=op1, accum_out=accum)
        P.add(eng, fn, reads, writes)

    def memset(eng, ap, val, writes):
        P.add(eng, lambda e, ap=ap, val=val: e.memset(ap, val), (), writes)

    def recip(out, in_, reads, writes):
        P.add("dve", lambda e, out=out, in_=in_: e.reciprocal(out=out, in_=in_), reads, writes)

    XVR = ["XV%d" % r for r in range(8)]
    HBR = ["hT%d_%d" % (m_, kh_) for m_ in range(8) for kh_ in range(2)]
    hTf = hT[:].rearrange("p k f -> p (k f)")

    def xsub(m):
        return XV[:, m * 1024:(m + 1) * 1024]

    def Vv(ri):
        return XV[:, ri * 4096:(ri + 1) * 4096].rearrange("p (q b c) -> p q b c", q=32, b=2)

    def Hv(ri):
        return hTf[:, ri * 4096:(ri + 1) * 4096].rearrange("p (q b c) -> p q b c", q=32, b=2)

    t32 = XV
    off = [0]

    def t32a(n):
        a = t32[:, off[0]:off[0] + n]
        off[0] += n
        return a

    offb = [0]

    def t32b(n):
        a = misc32[:, offb[0]:offb[0] + n]
        offb[0] += n
        return a

    idt32 = t32b(128)
    dma("sp", idt32, dident, (), ["pp_id"], "pp0")
    Are = t32b(32); Aim = t32b(32); Ldt = t32b(32)
    Bre = t32a(512); Bim = t32a(512); Cre = t32a(512); Cim = t32a(512)
    for i_, (t_, d_, nm) in enumerate([(Are, dAre, "Are"), (Aim, dAim, "Aim"), (Ldt, dLdt, "Ldt"), (Bre, dBre, "Bre"),
                                       (Bim, dBim, "Bim"), (Cre, dCre, "Cre"), (Cim, dCim, "Cim")]):
        dma("sp", t_, d_, (), [nm], "pp8_%d" % i_)
    Ws32 = ev32[:].rearrange("p a c -> p (a c)")
    tril32 = t32b(128)
    bsB32 = t32a(1024)
    dma("sp", Ws32, dWs, (), ["Ws32"], "pp5")
    dma("sp", tril32, dtril, (), ["tril"], "pp6")
    dma("sp", bsB32, dbsB, (), ["bsB"], "pp7")
    cp("act", identb[:], idt32, ["pp_id"], ["ident"])
    memset("dve", onesb[:], 1.0, ["ones"])
    memset("dve", epsc[:, 0:1], 1e-6, ["eps"])
    memset("dve", epsc[:, 1:2], 1e-5, ["eps"])
    dma("pool", gB0b[:], gB0, (), ["gB0"], "pp1")
    dma("pool", gB1b[:], gB1, (), ["gB1"], "pp2")
    for i_, (t_, d_, nm) in enumerate([(Dl, dDl, "Dl"), (bglu, dbglu, "bglu"), (lng, dlng, "lng"), (lnb, dlnb, "lnb")]):
        dma("sp", t_[:], d_, (), [nm], "pp4_%d" % i_)
    dtt = t32b(32); lre = t32b(32); xr = t32b(32); xi = t32b(32)
    act(dtt, Ldt, AF.Exp, ["Ldt"], ["dt"])
    ts("dve", lre, Are, -1e-4, None, ALU.min, None, ["Are"], ["lre"])
    tt("dve", xr, lre, dtt, ALU.mult, ["lre", "dt"], ["xr"])
    tt("dve", xi, Aim, dtt, ALU.mult, ["Aim", "dt"], ["xi"])
    TH = t32a(512); SC = t32a(512); MAG = t32a(256); Et = t32a(576)
    THv = TH.rearrange("p (e s q) -> p e s q", e=8, s=2)
    SCv = SC.rearrange("p (e s q) -> p e s q", e=8, s=2)
    MAGv = MAG.rearrange("p (e q) -> p e q", e=8)
    Etv = Et.rearrange("p (e r q) -> p e r q", e=9, r=2)
    for e_ in range(8):
        ts("dve", THv[:, e_, 0, :], xi, float(e_ + 1), None, ALU.mult, None, ["xi"], ["TH"])
        ts("dve", THv[:, e_, 1, :], xi, float(e_ + 1), 0.5 * PI, ALU.mult, ALU.add, ["xi"], ["TH"])
    MAGIC = 12582912.0
    ts("dve", SC, TH, 1.0 / (2 * PI), MAGIC, ALU.mult, ALU.add, ["TH"], ["SC"])
    ts("dve", SC, SC, -MAGIC, None, ALU.add, None, ["SC"], ["SC"])
    stt("dve", TH, SC, -2 * PI, TH, ALU.mult, ALU.add, ["SC", "TH"], ["TH"])
    ts("dve", TH, TH, -PI, PI, ALU.max, ALU.min, ["TH"], ["TH"])
    for e_ in range(8):
        act(MAGv[:, e_, :], xr, AF.Exp, ["xr"], ["MAG"], scale=float(e_ + 1))
    act(SC, TH, AF.Sin, ["TH"], ["SC"])
    memset("dve", Etv[:, 0, 0, :], 1.0, ["Et"])
    memset("dve", Etv[:, 0, 1, :], 0.0, ["Et"])
    tt("dve", Etv[:, 1:9, 0, :], MAGv, SCv[:, :, 1, :], ALU.mult, ["MAG", "SC"], ["Et"])
    tt("dve", Etv[:, 1:9, 1, :], MAGv, SCv[:, :, 0, :], ALU.mult, ["MAG", "SC"], ["Et"])
    cp("dve", ARI[:, 0, 0, :], Etv[:, 8, 0, :], ["Et"], ["ARI"])
    cp("dve", ARI[:, 0, 1, :], Etv[:, 8, 0, :], ["Et"], ["ARI"])
    ts("dve", ARI[:, 1, 0, :], Etv[:, 8, 1, :], -1.0, None, ALU.mult, None, ["Et"], ["ARI"])
    cp("dve", ARI[:, 1, 1, :], Etv[:, 8, 1, :], ["Et"], ["ARI"])
    nr = t32b(32); den = t32b(32); tA = t32b(32); tB = t32b(32); rr = t32b(32); rim = t32b(32)
    ni = Etv[:, 1, 1, :]
    ts("dve", nr, Etv[:, 1, 0, :], -1.0, None, ALU.add, None, ["Et"], ["nr"])
    tt("dve", den, lre, lre, ALU.mult, ["lre"], ["den"])
    tt("dve", tA, Aim, Aim, ALU.mult, ["Aim"], ["tA"])
    tt("dve", den, den, tA, ALU.add, ["den", "tA"], ["den"])
    recip(den, den, ["den"], ["den"])
    tt("dve", tA, nr, lre, ALU.mult, ["nr", "lre"], ["tA"])
    tt("dve", tB, ni, Aim, ALU.mult, ["Et", "Aim"], ["tB"])
    tt("dve", tA, tA, tB, ALU.add, ["tA", "tB"], ["tA"])
    tt("dve", rr, tA, den, ALU.mult, ["tA", "den"], ["rr"])
    tt("dve", tA, ni, lre, ALU.mult, ["Et", "lre"], ["tA"])
    tt("dve", tB, nr, Aim, ALU.mult, ["nr", "Aim"], ["tB"])
    tt("dve", tA, tA, tB, ALU.subtract, ["tA", "tB"], ["tA"])
    tt("dve", rim, tA, den, ALU.mult, ["tA", "den"], ["rim"])
    Bbr = t32a(512); Bbi = t32a(512); u1 = t32a(512); u2 = t32a(512); u3 = t32a(512); u4 = t32a(512)

    def v3(a):
        return a.rearrange("p (q h) -> p q h", q=32)

    def bq(a):
        return a.unsqueeze(2).broadcast_to([128, 32, 16])

    tt("dve", v3(u1), v3(Bre), bq(rr), ALU.mult, ["Bre", "rr"], ["u1"])
    tt("dve", v3(u2), v3(Bim), bq(rim), ALU.mult, ["Bim", "rim"], ["u2"])
    tt("dve", Bbr, u1, u2, ALU.subtract, ["u1", "u2"], ["Bbr"])
    tt("dve", v3(u1), v3(Bim), bq(rr), ALU.mult, ["Bim", "rr"], ["u1"])
    tt("dve", v3(u2), v3(Bre), bq(rim), ALU.mult, ["Bre", "rim"], ["u2"])
    tt("dve", Bbi, u1, u2, ALU.add, ["u1", "u2"], ["Bbi"])
    BBD = hT[:, 0:2, :].rearrange("p a (b c) -> p (a b) c", c=32).rearrange("p (q r) c -> p q r c", r=2)
    ZB0 = hT[:, 2:4, :].rearrange("p a (b c) -> p (a b) c", c=32).rearrange("p (q r) c -> p q r c", r=2)
    memset("pool", hT[:, 0:4, :], 0.0, ["BBD", "ZB0"])
    memset("pool", WOUT[:], 0.0, ["WOUT"])
    P.add("act", lambda e: e.memzero(A[:]), (), ["XBDlo"])
    P.add("act", lambda e: e.memzero(Bz[:]), (), ["XBDhi"])
    memset("pool", TK[:], 0.0, ["TK"])
    for gp in range(2):
        ps_ = slice(64 * gp, 64 * gp + 64)
        cs_ = slice(16 * gp, 16 * gp + 16)
        cp("dve", BBD[ps_, :, 0, cs_], v3(Bbr)[ps_], ["Bbr"], ["BBD"])
        cp("dve", BBD[ps_, :, 1, cs_], v3(Bbi)[ps_], ["Bbi"], ["BBD"])
        cp("dve", ZB0[ps_, :, 0, cs_], v3(Cre)[ps_], ["Cre"], ["ZB0"])
        ts("dve", ZB0[ps_, :, 1, cs_], v3(Cim)[ps_], -1.0, None, ALU.mult, None, ["Cim"], ["ZB0"])
    def XBD(q0, q1):
        buf = A if q0 < 16 else Bz
        v = buf[:].rearrange("p a (b c) -> p (a b) c", c=32).rearrange("p (q s r) c -> p q s r c", s=8, r=2)
        return v, (0 if q0 < 16 else 16)

    for e_ in range(9):
        Er = bq(Etv[:, e_, 0, :]); Ei = bq(Etv[:, e_, 1, :])
        if e_ >= 1:
            tau = e_ - 1
            tt("dve", v3(u1), v3(Cre), Er, ALU.mult, ["Cre", "Et"], ["u1"])
            tt("dve", v3(u2), v3(Cim), Ei, ALU.mult, ["Cim", "Et"], ["u2"])
            tt("dve", v3(u3), v3(Cre), Ei, ALU.mult, ["Cre", "Et"], ["u3"])
            tt("dve", v3(u4), v3(Cim), Er, ALU.mult, ["Cim", "Et"], ["u4"])
            for gp in range(2):
                ps_ = slice(64 * gp, 64 * gp + 64)
                cs_ = slice(16 * gp, 16 * gp + 16)
                tt("dve", WOUT[ps_, :, tau, 0, cs_], v3(u1)[ps_], v3(u2)[ps_], ALU.subtract, ["u1", "u2"], ["WOUT"])
                stt("dve", WOUT[ps_, :, tau, 1, cs_], v3(u3)[ps_], -1.0, v3(u4)[ps_], ALU.mult, ALU.subtract,
                    ["u3", "u4"], ["WOUT"])
        if e_ <= 7:
            sg_ = 7 - e_
            if e_ <= 0:
                en_, w1, w2, w3, w4 = "dve", u1, u2, u3, u4
                n1, n2, n3, n4 = "u1", "u2", "u3", "u4"
            else:
                BTf = BiasT[:].rearrange("p h t -> p (h t)")
                en_, w1, w2, w3, w4 = "pool", gBf32[:, 0:512], gBf32[:, 512:1024], BTf[:, 0:512], BTf[:, 512:1024]
                n1, n2, n3, n4 = "u5", "u6", "BiasT", "BiasT"
            tt(en_, v3(w1), v3(Bbr), Er, ALU.mult, ["Bbr", "Et"], [n1])
            tt(en_, v3(w2), v3(Bbi), Ei, ALU.mult, ["Bbi", "Et"], [n2])
            tt(en_, v3(w3), v3(Bbi), Er, ALU.mult, ["Bbi", "Et"], [n3])
            tt(en_, v3(w4), v3(Bbr), Ei, ALU.mult, ["Bbr", "Et"], [n4])
            for half in range(2):
                xv_, q0 = XBD(16 * half, 16 * half + 16)
                nm = "XBDlo" if half == 0 else "XBDhi"
                qs_ = slice(16 * half, 16 * half + 16)
                for gp in range(2):
                    ps_ = slice(64 * gp, 64 * gp + 64)
                    cs_ = slice(16 * gp, 16 * gp + 16)
                    tt(en_, xv_[ps_, :, sg_, 0, cs_], v3(w1)[ps_, qs_], v3(w2)[ps_, qs_], ALU.subtract, [n1, n2, nm], [nm + "_%d_%d" % (sg_, gp)])
                    tt(en_, xv_[ps_, :, sg_, 1, cs_], v3(w3)[ps_, qs_], v3(w4)[ps_, qs_], ALU.add, [n3, n4, nm], [nm + "_%d_%d" % (sg_, gp)])
    Wmb = xs[:, 0, :].rearrange("p (h s) -> p h s", h=8)
    tt("dve", Wmb, Ws32.rearrange("p (h s) -> p h s", h=8), tril32.unsqueeze(1).broadcast_to([128, 8, 128]),
       ALU.mult, ["Ws32", "tril"], ["xs0"])
    for h in range(8):
        tr(PT[:, h // 4, (h % 4) * 128:(h % 4 + 1) * 128], Wmb[:, h, :], ["xs0"], ["PT%d" % (h // 4)])
    for kh in range(2):
        cp("dve", WmT[:, 4 * kh:4 * kh + 4, :], PT[:, kh, 0:512].rearrange("p (h t) -> p h t", h=4), ["PT%d" % kh], ["WmT"])
    for kh in range(2):
        mm(PS[:, kh, :], onesb[:], WmT[:, 4 * kh:4 * kh + 4, :], True, True, ["ones", "WmT"], ["ps%d" % kh])
        for hh in range(4):
            h = 4 * kh + hh
            stt("dve", BiasT[:, h, :], PS[:, kh, hh * 128:(hh + 1) * 128], lnb[:, h:h + 1],
                bsB32.rearrange("p (h t) -> p h t", h=8)[:, h, :], ALU.mult, ALU.add,
                ["ps%d" % kh, "lnb", "bsB"], ["BiasT"])

    for j in range(8):
        bk = 2 + (j % 2)
        for i in range(4):
            q = 4 * j + i
            osl = PS[32 * i:32 * i + 32, bk, :]
            mm(osl[:, 0:32], BBD[:, q, 0, :], ZB0[:, q, 0, :], True, False, ["BBD", "ZB0"], ["ps%d" % bk], tp=(0, 32 * i), sgc=True)
            mm(osl[:, 0:32], BBD[:, q, 1, :], ZB0[:, q, 1, :], False, True, ["BBD", "ZB0"], ["ps%d" % bk], tp=(0, 32 * i), sgc=True)
            o2 = osl[:, 32:256].rearrange("p (k c) -> p k c", k=7)
            mm(o2, BBD[:, q, 0, :], WOUT[:, q, 0:7, 0, :], True, False, ["BBD", "WOUT"], ["ps%d" % bk], tp=(0, 32 * i), sgc=True)
            mm(o2, BBD[:, q, 1, :], WOUT[:, q, 0:7, 1, :], False, True, ["BBD", "WOUT"], ["ps%d" % bk], tp=(0, 32 * i), sgc=True)
        for i in range(4):
            sl_ = slice(32 * i, 32 * i + 32)
            cp("act" if i % 2 else "dve", TK[sl_, j, 1:8, sl_],
               PS[sl_, bk, 32:256].rearrange("p (k c) -> p k c", k=7), ["ps%d" % bk], ["TK"])
            stt("dve", TK[sl_, j, 0, sl_], idt32[sl_, sl_], Dl[sl_, j:j + 1], PS[sl_, bk, 0:32], ALU.mult, ALU.add,
                ["ps%d" % bk, "pp_id", "Dl"], ["TK"])
    cnt = 0
    for j in range(8):
        for g4 in range(4):
            bk = 2 + (cnt % 4)
            cnt += 1
            for s4 in range(4):
                sr = 4 * g4 + s4
                sg_, ri = sr // 2, sr % 2
                for i in range(4):
                    q = 4 * j + i
                    xv_, q0 = XBD(q, q + 1)
                    nm = "XBDlo" if q < 16 else "XBDhi"
                    mm(PS[32 * i:32 * i + 32, bk, s4 * 128:(s4 + 1) * 128], xv_[:, q - q0, sg_, ri, :], identb[:], True, True,
                       [nm, nm + "_%d_0" % sg_, nm + "_%d_1" % sg_, "ident"], ["ps%d" % bk], tp=(0, 32 * i))
            cp("act" if cnt % 2 else "dve", WIN[:, j, 4 * g4:4 * g4 + 4, :],
               PS[:, bk, :].rearrange("p (s c) -> p s c", s=4), ["ps%d" % bk], ["WIN"])
    memset("dve", hst[:], 0.0, ["hst"])
    dma("sp", gBf32[:], gBf, (), ["gBf"], "pp3", waitfor=["u5", "u6"])
    PREP_BUFS_A = []
    PREP_BUFS_B = []
    PREP_BUFS_H = []
    PREP_XV = []
    P.barrier()

    ring_use = [0]

    WBASE = {"w_in0": 0, "w_glu": 4, "w_out0": 6, "w_in1": 8, "w_out1": 14}
    cur_step = [0]

    preloaded = {}

    def preload_slab(W, s, step):
        old = cur_step[0]
        cur_step[0] = step
        preloaded[(W.tensor.name, s)] = load_slab(W, s)
        cur_step[0] = old

    def load_slab(W, s):
        if (W.tensor.name, s) in preloaded:
            return preloaded.pop((W.tensor.name, s))
        slot = ring_use[0] % 2
        ring_use[0] += 1
        wid = WBASE[W.tensor.name] + s
        if cur_step[0] == 0:
            dma("pool", ring[:, slot, :, :], W[:, 512 * s:512 * s + 512].rearrange("(k p) c -> p k c", p=128),
                (), ["ring%d" % slot], "ring%d" % slot)
            dma("sp", wscr[wid], ring[:, slot, :, :], ["ring%d" % slot], ["wscr%d" % wid], "wscr%d" % wid)
        else:
            dma("sp", ring[:, slot, :, :], wscr[wid], ["wscr%d" % wid], ["ring%d" % slot], "ring%d" % slot)
        return slot

    Astg = A[:, 0:4, :].rearrange("p a (b c) -> p (a b) c", c=512)
    ASTG_A = ["A%d_%d" % (j_, b_) for j_ in range(4) for b_ in range(2)]

    Astg0 = A[:, 4:8, :].rearrange("p a (b c) -> p (a b) c", c=512)
    ASTG_A0 = ["A%d_%d" % (j_, b_) for j_ in range(4, 8) for b_ in range(2)]

    def load_slab_stage(W, s, hi=False):
        wid = WBASE[W.tensor.name] + s
        dst, nm, key, wf = (Astg0, "Astage0", "astage0", ASTG_A0) if hi else (Astg, "Astage", "astage", ASTG_A)
        if cur_step[0] == 0:
            dma("pool", dst, W[:, 512 * s:512 * s + 512].rearrange("(k p) c -> p k c", p=128),
                (), [nm], key, waitfor=wf)
            dma("sp", wscr[wid], dst, [nm], ["wscr%d" % wid], "wscr%d" % wid)
        else:
            dma("sp", dst, wscr[wid], ["wscr%d" % wid], [nm], key, waitfor=wf)

    dbank = [0]

    def next_dbank():
        b_ = dbank[0] % 2
        dbank[0] += 1
        return b_

    sbank = [0]

    def next_sbank():
        b_ = 2 + (sbank[0] % 4)
        sbank[0] += 1
        return b_

    def hT_res(b):
        return ["hT%d_%d" % (m, kh) for m in range(4 * b, 4 * b + 4) for kh in range(2)]

    def rms_s1(m, gBb, so, tag):
        xm = xsub(m)
        xb = m % 2
        act(xs[:, xb, :], xm, AF.Square, [XVR[m]], ["xs%d" % xb], accum=stat[:, so + m:so + m + 1])
        act(stat[:, so + 8 + m:so + 9 + m], stat[:, so + m:so + m + 1], AF.Sqrt, ["xs%d" % xb], [tag + "a%d" % m],
            bias=epsc[:, 0:1], scale=1.0 / D)
        recip(stat[:, so + 16 + m:so + 17 + m], stat[:, so + 8 + m:so + 9 + m], [tag + "a%d" % m], [tag + "b%d" % m])
        stt("dve", xs[:, xb, :], xm, stat[:, so + 16 + m:so + 17 + m], gBb[:], ALU.mult, ALU.mult,
            [XVR[m], tag + "b%d" % m, "gB0", "gB1"], ["xs%d" % xb])

    def rms_s2(m):
        xb = m % 2
        for kh in range(2):
            for kk in range(4):
                k = 4 * kh + kk
                tr(PT[:, kh, kk * 128:(kk + 1) * 128], xs[:, xb, k * 128:(k + 1) * 128], ["xs%d" % xb], ["PT%d" % kh])
            cp("act" if kh == 0 else "dve", hT[:, 4 * kh:4 * kh + 4, m * 128:(m + 1) * 128],
               PT[:, kh, 0:512].rearrange("p (k t) -> p k t", k=4), ["PT%d" % kh],
               ["hT%d_%d" % (m, kh)], waitfor=["Hh0a", "Hh0b", "Hh1a", "Hh1b"])

    def rms_order():
        o = [("s1", 0), ("s1", 1)]
        for m in range(8):
            o.append(("s2", m))
            if m + 2 < 8:
                o.append(("s1", m + 2))
        return o

    def run_rms(order, ptr, upto_m, gBb, so, tag):
        while ptr < len(order) and order[ptr][1] <= upto_m:
            kind, m = order[ptr]
            if kind == "s1":
                rms_s1(m, gBb, so, tag)
            else:
                rms_s2(m)
            ptr += 1
        return ptr

    def load_x_m(n, m):
        b, blk = m // 4, m % 4
        dma("sp", xsub(m), x2[b, TS * n + 128 * blk:TS * n + 128 * (blk + 1), :], (), [XVR[m]], "xm%d" % m,
            waitfor=["Vh0", "Vh1"])

    def reload_x(n):
        for b in range(2):
            dma("sp", XV[:, 4096 * b:4096 * (b + 1)].rearrange("p (m d) -> p m d", m=4),
                x2[b, TS * n:TS * (n + 1), :].rearrange("(m p) d -> p m d", p=128),
                (), [XVR[m] for m in range(4 * b, 4 * b + 4)], "x%d" % b, waitfor=["Vh0", "Vh1"])

    def dense_fm(W, s_list, evac, src=None, b_outer=False, hooks=None):
        if b_outer:
            assert len(s_list) == 2 and src is None
            slots = [load_slab(W, s) for s in s_list]
            tix = 0
            for b in range(2):
                for si, s in enumerate(s_list):
                    for jj in range(4):
                        if hooks and tix in hooks:
                            hooks[tix]()
                        tix += 1
                        bk = next_dbank()
                        for k in range(8):
                            mm(PS[:, bk, :], ring[:, slots[si], k, jj * 128:(jj + 1) * 128], hT[:, k, b * 512:(b + 1) * 512],
                               k == 0, k == 7, ["ring%d" % slots[si]] + hT_res(b), ["ps%d" % bk])
                        evac(s, jj, b, bk)
            return
        for s in s_list:
            slot = load_slab(W, s)
            for jj in range(4):
                for b in range(2):
                    bk = next_dbank()
                    for k in range(8):
                        if src is None:
                            rhs, rres = hT[:, k, b * 512:(b + 1) * 512], hT_res(b)
                        else:
                            rhs, rres = A[:, k, b * 512:(b + 1) * 512], ["A%d_%d" % (k, b)]
                        mm(PS[:, bk, :], ring[:, slot, k, jj * 128:(jj + 1) * 128], rhs,
                           k == 0, k == 7, ["ring%d" % slot] + rres, ["ps%d" % bk])
                    evac(s, jj, b, bk)

    def tm_tiles(W, post_m, pre=None, stage_both=False, skip_loads=False):
        if stage_both:
            if not skip_loads:
                load_slab_stage(W, 0, hi=True)
            slot0 = None
        else:
            slot0 = load_slab(W, 0)
        if not skip_loads:
            load_slab_stage(W, 1)
        if pre is not None:
            pre()
        yield
        for m in range(8):
            b = m // 4
            for dh in range(2):
                bk = next_dbank()
                for k in range(8):
                    if dh == 0 and slot0 is None:
                        rhs, rres = Astg0[:, k, :], "Astage0"
                    elif dh == 0:
                        rhs, rres = ring[:, slot0, k, :], "ring%d" % slot0
                    else:
                        rhs, rres = Astg[:, k, :], "Astage"
                    mm(PS[:, bk, :], Bz[:, k, m * 128:(m + 1) * 128], rhs, k == 0, k == 7,
                       [rres, "B%d_%d" % (k, b)], ["ps%d" % bk])
                tt("dve", xsub(m)[:, dh * 512:(dh + 1) * 512], PS[:, bk, :], xsub(m)[:, dh * 512:(dh + 1) * 512], ALU.add,
                   ["ps%d" % bk, XVR[m]], [XVR[m]])
                if dh == 1 and m >= 1:
                    post_m(m - 1)
                yield
        post_m(7)

    def dense_tm_out(W, post_m):
        slots = [load_slab(W, 0), load_slab(W, 1)]
        for m in range(8):
            b = m // 4
            for dh in range(2):
                bk = next_dbank()
                for k in range(8):
                    mm(PS[:, bk, :], Bz[:, k, m * 128:(m + 1) * 128], ring[:, slots[dh], k, :], k == 0, k == 7,
                       ["ring%d" % slots[dh], "B%d_%d" % (k, b)], ["ps%d" % bk])
                tt("dve", xsub(m)[:, dh * 512:(dh + 1) * 512], PS[:, bk, :], xsub(m)[:, dh * 512:(dh + 1) * 512], ALU.add,
                   ["ps%d" % bk, XVR[m]], [XVR[m]])
            if m >= 1:
                post_m(m - 1)
        post_m(7)

    AR2r = ARI[:, 0, 0, :].unsqueeze(2).broadcast_to([128, 32, 2])
    ARn = ARI[:, 1, 0, :].unsqueeze(2).broadcast_to([128, 32, 2])
    ARp = ARI[:, 1, 1, :].unsqueeze(2).broadcast_to([128, 32, 2])

    def h4(a):
        return a.rearrange("p r (q b) -> p r q b", b=2)

    def h3(a):
        return a.rearrange("p (q b) -> p q b", b=2)

    Hall = hTf.rearrange("p (r q b c) -> p r q b c", r=2, q=32, b=2)
    Vall = XV[:, :].rearrange("p (r q b c) -> p r q b c", r=2, q=32, b=2)

    def ttn(out, in0, in1, op, reads, writes, waitfor=()):
        P.add("dve", lambda e, out=out, in0=in0, in1=in1, op=op: e.tensor_tensor(out=out, in0=in0, in1=in1, op=op),
              reads, writes, waitfor=waitfor, noself=True)

    L0_ORDER = rms_order()
    l0_ptr = 0
    for m in range(8):
        load_x_m(0, m)

    for n in range(NSTEP):
        cur_step[0] = n
        hooks = None
        if n > 0:
            pend = L0_ORDER[l0_ptr:]
            sched = {}
            slot_ = 2
            st = {"p": l0_ptr}

            def mk(upto_idx):
                def f():
                    while st["p"] <= upto_idx:
                        kind, m = L0_ORDER[st["p"]]
                        if kind == "s1":
                            rms_s1(m, gB0b, 0, "r0")
                        else:
                            rms_s2(m)
                        st["p"] += 1
                return f
            idxs = list(range(l0_ptr, len(L0_ORDER)))
            lead = l0_ptr
            while lead < len(L0_ORDER) and L0_ORDER[lead][0] == "s1":
                lead += 1
            mk(lead - 1)()
            rest = list(range(st["p"], len(L0_ORDER)))
            hooks = {}
            for r_i, li in enumerate(rest):
                hooks[min(7, 2 + 3 * r_i)] = mk(li)
            hooks[8] = mk(len(L0_ORDER) - 1)
            l0_ptr = len(L0_ORDER)
        else:
            l0_ptr = run_rms(L0_ORDER, l0_ptr, 7, gB0b, 0, "r0")

        def ev_in0(s, jj, b, bk):
            if s < 2:
                j = 4 * s + jj
                cp("dve", A[:, j, b * 512:(b + 1) * 512], PS[:, bk, :], ["ps%d" % bk], ["A%d_%d" % (j, b)], waitfor=["Astage", "Astage0"])
            else:
                j = 4 * (s - 2) + jj
                act(Bz[:, j, b * 512:(b + 1) * 512], PS[:, bk, :], AF.Silu, ["ps%d" % bk], ["B%d_%d" % (j, b)])
        dense_fm(w_in0, [0, 1], ev_in0, b_outer=True, hooks=hooks)

        for j in range(8):
            for ri in range(2):
                col = ((j % 2) * 2 + ri) * 128
                for sg_ in range(8):
                    for i in range(4):
                        mm(PS[:, 2 + i, col:col + 128], WIN[32 * i:32 * i + 32, j, 2 * sg_ + ri, :],
                           A[32 * i:32 * i + 32, j, :].rearrange("p (c t) -> p c t", t=8)[:, :, sg_],
                           sg_ == 0, sg_ == 7, ["WIN", "A%d_0" % j, "A%d_1" % j], ["ps%d" % (2 + i)], tp=(32 * i, 0))
            for i in range(4):
                q = 4 * j + i
                for ri in range(2):
                    col = ((j % 2) * 2 + ri) * 128
                    cp("act" if i % 2 else "dve", Vv(ri)[:, q, :, :],
                       PS[:, 2 + i, col:col + 128].rearrange("p (b c) -> p b c", b=2),
                       ["ps%d" % (2 + i)], [XVR[4 * ri + q // 8]])
        for ct in range(64):
            if ct == 0:
                hr_, hi_ = h3(hst[:, 0, :]), h3(hst[:, 1, :])
                hres = ["hst"]
            else:
                hr_, hi_ = Vall[:, 0, :, :, ct - 1], Vall[:, 1, :, :, ct - 1]
                hres = ["Vh%d" % ((ct - 1) // 32)]
            vres = "Vh%d" % (ct // 32)
            wf = XVR if ct == 0 else ()
            ttn(h3(rt[:, 0, 0, :]), hr_, AR2r, ALU.mult, hres + ["ARI"], ["rtA"], waitfor=wf)
            ttn(h3(rt[:, 0, 1, :]), hi_, AR2r, ALU.mult, hres + ["ARI"], ["rtB"])
            ttn(h3(rt[:, 1, 0, :]), hi_, ARn, ALU.mult, hres + ["ARI"], ["rtC"])
            ttn(h3(rt[:, 1, 1, :]), hr_, ARp, ALU.mult, hres + ["ARI"], ["rtD"])
            ttn(rt[:, 2, 0, :], rt[:, 0, 0, :], rt[:, 1, 0, :], ALU.add, ["rtA", "rtC"], ["rtE"])
            ttn(rt[:, 2, 1, :], rt[:, 0, 1, :], rt[:, 1, 1, :], ALU.add, ["rtB", "rtD"], ["rtF"])
            ttn(Vall[:, 0, :, :, ct], h3(rt[:, 2, 0, :]), Vall[:, 0, :, :, ct], ALU.add, ["rtE", vres], [vres])
            ttn(Vall[:, 1, :, :, ct], h3(rt[:, 2, 1, :]), Vall[:, 1, :, :, ct], ALU.add, ["rtF", vres], [vres])
        dense_fm(w_in0, [2, 3], ev_in0)

        glu_slots = []
        park = ([(xs[:, a_, 256 * c_:256 * c_ + 256], "xs%d" % a_) for a_ in range(2) for c_ in range(4)]
                + [(evb[:, a_, 256 * c_:256 * c_ + 256], "evb%d" % a_) for a_ in range(2) for c_ in range(2)]
                + [(ev32[:, a_, 256 * c_:256 * c_ + 256], "ev32_%d" % a_) for a_ in range(2) for c_ in range(2)])
        glu_pending = []

        def glu_half(h2):
            if not glu_slots:
                glu_slots.extend([load_slab(w_glu, 0), load_slab(w_glu, 1)])
            cnt_ = 0
            for s_ in range(2):
                slot = glu_slots[s_]
                for jj in range(4):
                    j = 4 * s_ + jj
                    for b in range(2):
                        bk = next_dbank()
                        c0 = b * 512 + 256 * h2
                        for k in range(8):
                            mm(PS[:, bk, 0:256], ring[:, slot, k, jj * 128:(jj + 1) * 128], A[:, k, c0:c0 + 256],
                               k == 0, k == 7, ["ring%d" % slot, "A%d_%d" % (k, b)], ["ps%d" % bk])
                        if h2 == 0:
                            dst, nm = park[cnt_]
                            pn = "park%d" % cnt_
                            act(dst, PS[:, bk, 0:256], AF.Sigmoid, ["ps%d" % bk, "bglu"], [pn, nm], bias=bglu[:, j:j + 1])
                            glu_pending.append((dst, [pn, nm], j, b, c0))
                        else:
                            eb = cnt_ % 2
                            act(evb[:, eb, 0:256], PS[:, bk, 0:256], AF.Sigmoid, ["ps%d" % bk, "bglu"], ["evb%d" % eb],
                                bias=bglu[:, j:j + 1])
                            tt("dve", Bz[:, j, c0:c0 + 256], evb[:, eb, 0:256], Bz[:, j, c0:c0 + 256], ALU.mult,
                               ["evb%d" % eb, "B%d_%d" % (j, b)], ["B%d_%d" % (j, b)])
                        cnt_ += 1

        def glu_flush():
            for dst, nms, j, b, c0 in glu_pending:
                tt("dve", Bz[:, j, c0:c0 + 256], dst, Bz[:, j, c0:c0 + 256], ALU.mult,
                   nms + ["B%d_%d" % (j, b)], ["B%d_%d" % (j, b)])
            del glu_pending[:]

        for h2 in range(2):
            if h2 == 1:
                glu_half(0)
            if h2 == 0:
                cp("act", Hall[:, :, :, :, 0], h4(hst[:, 0:2, :]), ["hst"], HBR + ["Hh0a", "Hh0b"])
                cp("act", Hall[:, :, 0:16, :, 1:32], Vall[:, :, 0:16, :, 0:31], ["Vh0"], HBR + ["Hh0a"])
                cp("act", Hall[:, :, 16:32, :, 1:32], Vall[:, :, 16:32, :, 0:31], ["Vh0"], HBR + ["Hh0b"])
            else:
                cp("act", Hall[:, :, 0:16, :, 32:64], Vall[:, :, 0:16, :, 31:63], ["Vh0", "Vh1"], HBR + ["Hh1a"])
                cp("act", Hall[:, :, 16:32, :, 32:64], Vall[:, :, 16:32, :, 31:63], ["Vh0", "Vh1"], HBR + ["Hh1b"])
                cp("dve", h4(hst[:, 0:2, :]), Vall[:, :, :, :, 63], ["Vh1"], ["hst"])
                glu_flush()
                reload_x(n)
            for jg in range(2):
                bks = {}
                for j in range(4 * jg, 4 * jg + 4):
                    bk = next_sbank()
                    bks[j] = bk
                    ures = ["A%d_0" % j, "A%d_1" % j]
                    uv = A[:, j, :].rearrange("p (b c t) -> p b c t", b=2, t=8)[:, :, 32 * h2:32 * h2 + 32, :]
                    ov = PS[:, bk, :].rearrange("p (b c t) -> p b c t", b=2, t=8)
                    for k in range(8):
                        mm(ov[:, :, :, k:8], TK[:, j, k, :], uv[:, :, :, 0:8 - k], k == 0, False, ["TK"] + ures, ["ps%d" % bk], sgc=True)
                for j in range(4 * jg, 4 * jg + 4):
                    bk = bks[j]
                    ures = ["A%d_0" % j, "A%d_1" % j]
                    for tau in range(8):
                        for ri in range(2):
                            for i in range(4):
                                q = 4 * j + i
                                last = (i == 3 and tau == 7 and ri == 1)
                                mm(PS[32 * i:32 * i + 32, bk, :].rearrange("p (b c t) -> p b c t", b=2, t=8)[:, :, :, tau],
                                   WOUT[:, q, tau, ri, :], Hv(ri)[:, q, :, 32 * h2:32 * h2 + 32], False, last,
                                   ["WOUT", "Hh%d%s" % (h2, "a" if q < 16 else "b")], ["ps%d" % bk], tp=(0, 32 * i), sgc=True)
                    av_ = A[:, j, :].rearrange("p (b x) -> p b x", b=2)[:, :, 256 * h2:256 * h2 + 256]
                    bv_ = Bz[:, j, :].rearrange("p (b x) -> p b x", b=2)[:, :, 256 * h2:256 * h2 + 256]
                    act(av_, PS[:, bk, :].rearrange("p (b x) -> p b x", b=2), AF.Gelu, ["ps%d" % bk], ures)
                    tt("dve", bv_, bv_, av_, ALU.mult, ures + ["B%d_0" % j, "B%d_1" % j], ["B%d_0" % j, "B%d_1" % j])
        glu_half(1)
        L1_ORDER = rms_order()
        l1 = [0]

        def post0(m):
            l1[0] = run_rms(L1_ORDER, l1[0], m, gB1b, 0, "r1")
        for _ in tm_tiles(w_out0, post0, pre=lambda n=n: preload_slab(w_in1, 0, n)):
            pass
        l1[0] = run_rms(L1_ORDER, l1[0], 7, gB1b, 0, "r1")

        def ev_in1(s, jj, b, bk):
            if s < 2:
                j = 4 * s + jj
                act(A[:, j, b * 512:(b + 1) * 512], PS[:, bk, :], AF.Gelu, ["ps%d" % bk], ["A%d_%d" % (j, b)], waitfor=["Astage", "Astage0"])
            else:
                j = 4 * (s - 4) + jj
                act(Bz[:, j, b * 512:(b + 1) * 512], PS[:, bk, :], AF.Silu, ["ps%d" % bk], ["B%d_%d" % (j, b)])
                tt("dve", Bz[:, j, b * 512:(b + 1) * 512], Bz[:, j, b * 512:(b + 1) * 512], A[:, j, b * 512:(b + 1) * 512], ALU.mult,
                   ["B%d_%d" % (j, b), "A%d_%d" % (j, b)], ["B%d_%d" % (j, b)])
        dense_fm(w_in1, [0, 1, 4, 5], ev_in1)
        vslots = [load_slab(w_in1, 2), load_slab(w_in1, 3)]
        load_slab_stage(w_out1, 0, hi=True)
        load_slab_stage(w_out1, 1)
        vb = [0]
        ln_deferred = []

        def ln_stats(g_):
            so = 48 + 24 * g_
            ms_ = slice(4 * g_, 4 * g_ + 4)
            S12v = S12[:].rearrange("p m w h -> p w m h")
            tg = "ln%d" % g_
            tt("dve", stat[:, so:so + 8].rearrange("p (w m) -> p w m", w=2), S12v[:, :, ms_, 0], S12v[:, :, ms_, 1], ALU.add,
               ["evb0", "evb1"] + HBR, [tg + "A"])
            ts("dve", stat[:, so:so + 8], stat[:, so:so + 8], 1.0 / D, None, ALU.mult, None, [tg + "A"], [tg + "A"])
            tt("dve", stat[:, so + 8:so + 12], stat[:, so:so + 4], stat[:, so:so + 4], ALU.mult, [tg + "A"], [tg + "B"])
            tt("dve", stat[:, so + 12:so + 16], stat[:, so + 4:so + 8], stat[:, so + 8:so + 12], ALU.subtract,
               [tg + "A", tg + "B"], [tg + "C"])
            act(stat[:, so + 16:so + 20], stat[:, so + 12:so + 16], AF.Sqrt, [tg + "C"], [tg + "D"], bias=epsc[:, 1:2])
            recip(stat[:, so + 20:so + 24], stat[:, so + 16:so + 20], [tg + "D"], [tg + "E"])
            for mm_ in range(4 * g_, 4 * g_ + 4):
                blkv = hT[:, :, mm_ * 128:(mm_ + 1) * 128]
                ts("dve", blkv, blkv, stat[:, so + (mm_ % 4):so + (mm_ % 4) + 1], stat[:, so + 20 + (mm_ % 4):so + 21 + (mm_ % 4)],
                   ALU.subtract, ALU.mult,
                   ["hT%d_0" % mm_, "hT%d_1" % mm_, tg + "A", tg + "E"], ["hT%d_0" % mm_, "hT%d_1" % mm_])

        for m in range(8):
            bks = []
            for half in range(2):
                bk = [0, 1, 4, 5][vb[0] % 4]
                vb[0] += 1
                bks.append(bk)
                for k in range(8):
                    mm(PS[:, bk, :], hT[:, k, m * 128:(m + 1) * 128], ring[:, vslots[half], k, :], k == 0, k == 7,
                       ["ring%d" % vslots[half], "hT%d_%d" % (m, k // 4)], ["ps%d" % bk])
            for half in range(2):
                bk = bks[half]
                gdst = hT[:, 4 * half:4 * half + 4, m * 128:(m + 1) * 128]
                act(gdst, PS[:, bk, :].rearrange("p (h c) -> p h c", h=4), AF.Gelu, ["ps%d" % bk], ["hT%d_%d" % (m, half)],
                    accum=S12[:, m, 0, half:half + 1])
                stt("dve", evb[:, half, :].rearrange("p (h c) -> p h c", h=4), gdst, 1.0, gdst, ALU.mult, ALU.mult,
                    ["hT%d_%d" % (m, half)], ["evb%d" % half], accum=S12[:, m, 1, half:half + 1])
            if m == 3:
                ln_stats(0)
            if m == 7:
                ln_deferred.append(lambda: ln_stats(1))
        def mix_tile(h, b, idx):
            bk = next_sbank()
            for blk in range(4):
                m = 4 * b + blk
                mm(PS[:, bk, blk * 128:(blk + 1) * 128], hT[:, h, m * 128:(m + 1) * 128], WmT[:, h, :], True, True,
                   ["hT%d_%d" % (m, h // 4), "WmT"], ["ps%d" % bk])
            eb = idx % 2
            stt("dve", ev32[:, eb, :].rearrange("p (k t) -> p k t", k=4), PS[:, bk, :].rearrange("p (k t) -> p k t", k=4),
                lng[:, h:h + 1], BiasT[:, h, :].unsqueeze(1).broadcast_to([128, 4, 128]), ALU.mult, ALU.add,
                ["ps%d" % bk, "lng", "BiasT"], ["ev32_%d" % eb])
            tt("pool" if idx % 2 else "dve", Bz[:, h, b * 512:(b + 1) * 512], ev32[:, eb, :], Bz[:, h, b * 512:(b + 1) * 512], ALU.mult,
               ["ev32_%d" % eb, "B%d_%d" % (h, b)], ["B%d_%d" % (h, b)])
        l0_ptr = 0
        nxt = [0]

        def post1(m, n=n):
            xm = xsub(m)
            xb = m % 2
            fo = 24
            act(evb[:, xb, :].rearrange("p (a c) -> p a c", a=1)[:, 0, :] if False else ev32[:, xb, :], xm[:, 0:512], AF.Square,
                [XVR[m]], ["ev32_%d" % xb], accum=stat[:, fo + m:fo + m + 1])
            act(evb[:, xb, :], xm[:, 512:1024], AF.Square, [XVR[m]], ["evb%d" % xb], accum=stat[:, fo + 8 + m:fo + 9 + m])
            tt("dve", stat[:, fo + m:fo + m + 1], stat[:, fo + m:fo + m + 1], stat[:, fo + 8 + m:fo + 9 + m], ALU.add,
               ["ev32_%d" % xb, "evb%d" % xb], ["fa%d" % m])
            act(stat[:, fo + 8 + m:fo + 9 + m], stat[:, fo + m:fo + m + 1], AF.Sqrt, ["fa%d" % m], ["fb%d" % m],
                bias=epsc[:, 0:1], scale=1.0 / D)
            recip(stat[:, fo + 16 + m:fo + 17 + m], stat[:, fo + 8 + m:fo + 9 + m], ["fb%d" % m], ["fc%d" % m])
            tt("pool", xm, xm, gBf32[:], ALU.mult, [XVR[m], "gBf", "ev32_%d" % xb, "evb%d" % xb], [XVR[m]])
            act(xm, xm, AF.Copy, [XVR[m], "fc%d" % m], [XVR[m]], scale=stat[:, fo + 16 + m:fo + 17 + m])
            b, blk = m // 4, m % 4
            dma("sp", y2[b, TS * n + 128 * blk:TS * n + 128 * (blk + 1), :], xm, [XVR[m]], ["out%d" % m], "out%d" % m)
            if n + 1 < NSTEP:
                load_x_m(n + 1, m)
                nxt[0] = run_rms(L0_ORDER, nxt[0], m - 2, gB0b, 0, "r0")
        if n + 1 < NSTEP:
            preload_slab(w_in0, 0, n + 1)
            preload_slab(w_in0, 1, n + 1)
        g_out = tm_tiles(w_out1, post1, stage_both=True, skip_loads=True)
        next(g_out)
        for h in range(8):
            mix_tile(h, 0, h)
        for f_ in ln_deferred:
            f_()
        for h in range(8):
            next(g_out)
            mix_tile(h, 1, 8 + h)
        for _ in g_out:
            pass
        l0_ptr = nxt[0]
    P.add("sp", None, ["out%d" % m for m in range(8)], ())

    P.finalize()
    esem = {e: es.enter_context(nc.semaphore("sem_" + e)) for e in Prog.ENG}
    dsem = {k: es.enter_context(nc.semaphore("dsem_" + k)) for k in P.dmacnt}
    block = es.enter_context(nc.Block())

    @block.tensor
    def _(e):
        P.emit_engine("pe", e, esem, dsem)

    @block.scalar
    def _(e):
        P.emit_engine("act", e, esem, dsem)

    @block.vector
    def _(e):
        P.emit_engine("dve", e, esem, dsem)

    @block.gpsimd
    def _(e):
        P.emit_engine("pool", e, esem, dsem)

    @block.sync
    def _(e):
        P.emit_engine("sp", e, esem, dsem)

    es.close()
    return nc


def host_inputs(inp):
    f = lambda a: np.ascontiguousarray(np.asarray(a, dtype=np.float32))
    shared = {}
    shared["gB0"] = f(np.broadcast_to(inp["norm_g"][0][None, :], (128, D)))
    shared["gB1"] = f(np.broadcast_to(inp["norm_g"][1][None, :], (128, D)))
    shared["gBf"] = f(np.broadcast_to(inp["final_g"][None, :], (128, D)))
    shared["w_in0"] = f(inp["s5_w_in"][0])
    shared["w_glu"] = f(inp["s5_w_glu"][0])
    shared["w_out0"] = f(inp["s5_w_out"][0])
    shared["w_in1"] = f(inp["sgu_w_in"][0])
    shared["w_out1"] = f(inp["sgu_w_out"][0])

    def gp_layout(a):
        a = np.asarray(a)
        rest = a.shape[2:]
        a = a.reshape((32, 2, 64) + rest)
        a = np.moveaxis(a, 0, 2)
        return a.reshape((128, 32) + rest)
    shared["Are"] = f(gp_layout(inp["s5_A_re"][0]))
    shared["Aim"] = f(gp_layout(inp["s5_A_im"][0]))
    shared["Ldt"] = f(gp_layout(np.broadcast_to(np.asarray(inp["s5_log_dt"][0])[:, None], (64, 64))))
    shared["Bre"] = f(gp_layout(inp["s5_B_re"][0]).reshape(128, 512))
    shared["Bim"] = f(gp_layout(inp["s5_B_im"][0]).reshape(128, 512))
    shared["Cre"] = f(gp_layout(np.transpose(inp["s5_C_re"][0], (0, 2, 1))).reshape(128, 512))
    shared["Cim"] = f(gp_layout(np.transpose(inp["s5_C_im"][0], (0, 2, 1))).reshape(128, 512))
    shared["Dl"] = f(np.asarray(inp["s5_D"][0]).reshape(8, 128).T)
    shared["bglu"] = f(np.asarray(inp["s5_b_glu"][0]).reshape(8, 128).T)
    shared["Ws"] = f(np.transpose(inp["sgu_w_s"][0], (1, 0, 2)).reshape(128, 1024))
    shared["tril"] = f(np.tril(np.ones((128, 128), np.float32)))
    shared["lng"] = f(np.asarray(inp["sgu_ln_g"][0]).reshape(8, 128).T)
    shared["lnb"] = f(np.asarray(inp["sgu_ln_b"][0]).reshape(8, 128).T)
    shared["bsB"] = f(np.broadcast_to(np.asarray(inp["sgu_b_s"][0]).reshape(1, 1024), (128, 1024)))
    shared["ident"] = f(np.eye(128, dtype=np.float32))
    return shared


_NC_CACHE = {}


def kernel(**inputs):
    x = np.asarray(inputs["x"], dtype=np.float32)
    shared = host_inputs(inputs)
    if "nc" not in _NC_CACHE:
        _NC_CACHE["nc"] = build_nc()
    nc = _NC_CACHE["nc"]
    in_maps = []
    for c in range(NCORES):
        m = dict(shared)
        m["x2"] = np.ascontiguousarray(x[2 * c:2 * c + 2])
        in_maps.append(m)
    res = run_bass_kernel_spmd(nc, in_maps, core_ids=list(range(NCORES)))
    out = np.concatenate([np.asarray(r["y2"]) for r in res.results], axis=0)
    return out.astype(np.float32)
```
